# Optimizing a Trainium2 kernel written in Bass

```python
import jax, jax.numpy as jnp
from jax import lax
import numpy as np

D_MODEL = 1024
BATCH = 16
SEQ = 4096
DEPTH = 4

HEAD_DIM = 64
ROPE_THETA = 10000.0
NORM_EPS = 1e-6
BLOCK = 128
NEG_INF = -1e30

A_Q_HEADS = 6
A_KV_HEADS = 2
A_WINDOW = 128
A_WIDTH = A_Q_HEADS * HEAD_DIM

B_HEADS = 6
B_Q_RANK = 384
B_KV_RANK = 256
B_NOPE = 64
B_ROPE = 32
B_V = 64
B_WIDTH = B_HEADS * B_V

C_GROUPS = 4
C_GROUP_DIM = 64
C_WIDTH = C_GROUPS * C_GROUP_DIM
C_CHUNK = 128

MIX_WIDTH = A_WIDTH + B_WIDTH + C_WIDTH
IN_SIZES = (A_WIDTH, A_KV_HEADS * HEAD_DIM, A_KV_HEADS * HEAD_DIM,
            B_Q_RANK, B_KV_RANK, B_ROPE, C_WIDTH, C_WIDTH)
IN_COLS = A_WIDTH + 2 * A_KV_HEADS * HEAD_DIM + B_Q_RANK + B_KV_RANK + B_ROPE + 2 * C_WIDTH

FFN_HIDDEN = ((8 * D_MODEL // 3 + 255) // 256) * 256
N_MOD = 6

kernel_name = "hybrid_parallel_swa_mla_sgu_block"


def rms_norm(x, w):
    xf = x.astype(jnp.float32)
    y = xf * lax.rsqrt(jnp.mean(xf * xf, axis=-1, keepdims=True) + NORM_EPS)
    return (y * w.astype(jnp.float32)).astype(x.dtype)


def layer_norm(x, w, b):
    xf = x.astype(jnp.float32)
    mu = jnp.mean(xf, axis=-1, keepdims=True)
    var = jnp.mean(jnp.square(xf - mu), axis=-1, keepdims=True)
    y = (xf - mu) * lax.rsqrt(var + NORM_EPS)
    return (y * w.astype(jnp.float32) + b.astype(jnp.float32)).astype(x.dtype)


def rope_tables(positions, dim):
    inv = 1.0 / (ROPE_THETA ** (jnp.arange(0, dim, 2, dtype=jnp.float32) / dim))
    ang = positions.astype(jnp.float32)[..., None] * inv
    return jnp.cos(ang), jnp.sin(ang)


def apply_rope(x, cos, sin):
    x1, x2 = jnp.split(x, 2, axis=-1)
    c = cos[:, :, None, :].astype(x.dtype)
    s = sin[:, :, None, :].astype(x.dtype)
    return jnp.concatenate([x1 * c - x2 * s, x2 * c + x1 * s], axis=-1)


def sliding_window_gqa(q, k, v, sinks):
    b_, s_, hq, d = q.shape
    g = hq // A_KV_HEADS
    nb = s_ // BLOCK
    qb = q.reshape(b_, nb, BLOCK, A_KV_HEADS, g, d)
    pad = ((0, 0), (BLOCK, 0), (0, 0), (0, 0))
    kb = jnp.pad(k, pad).reshape(b_, nb + 1, BLOCK, A_KV_HEADS, d)
    vb = jnp.pad(v, pad).reshape(b_, nb + 1, BLOCK, A_KV_HEADS, d)
    kcat = jnp.concatenate([kb[:, :-1], kb[:, 1:]], axis=2)
    vcat = jnp.concatenate([vb[:, :-1], vb[:, 1:]], axis=2)
    scores = jnp.einsum('bnqhgd,bnkhd->bnhgqk', qb, kcat).astype(jnp.float32) * (d ** -0.5)
    qi = jnp.arange(BLOCK)[:, None]
    kj = jnp.arange(2 * BLOCK)[None, :]
    rel = qi + BLOCK - kj
    band = (rel >= 0) & (rel < A_WINDOW)
    not_pad = (jnp.arange(nb)[:, None, None] > 0) | (kj[None] >= BLOCK)
    mask = (band[None] & not_pad)[None, :, None, None]
    scores = jnp.where(mask, scores, NEG_INF)
    sink = sinks.astype(jnp.float32).reshape(A_KV_HEADS, g)[None, None, :, :, None, None]
    m = jnp.maximum(jnp.max(scores, axis=-1, keepdims=True), sink)
    p = jnp.exp(scores - m)
    p = p / (jnp.sum(p, axis=-1, keepdims=True) + jnp.exp(sink - m))
    out = jnp.einsum('bnhgqk,bnkhd->bnqhgd', p.astype(v.dtype), vcat)
    return out.reshape(b_, s_, hq * d)


def mla(c_q, c_kv, k_rope, q_norm_w, w_uq, kv_norm_w, w_ukv, cos, sin):
    b_, s_, _ = c_q.shape
    q = (rms_norm(c_q, q_norm_w) @ w_uq).reshape(b_, s_, B_HEADS, B_NOPE + B_ROPE)
    q_nope, q_rope = q[..., :B_NOPE], q[..., B_NOPE:]
    q_rope = apply_rope(q_rope, cos, sin)
    kv = (rms_norm(c_kv, kv_norm_w) @ w_ukv).reshape(b_, s_, B_HEADS, B_NOPE + B_V)
    k_nope, v = kv[..., :B_NOPE], kv[..., B_NOPE:]
    k_r = apply_rope(k_rope[:, :, None, :], cos, sin)[:, :, 0, :]
    scale = (B_NOPE + B_ROPE) ** -0.5
    nb = s_ // BLOCK
    qn = q_nope.reshape(b_, nb, BLOCK, B_HEADS, B_NOPE).transpose(1, 0, 2, 3, 4)
    qr = q_rope.reshape(b_, nb, BLOCK, B_HEADS, B_ROPE).transpose(1, 0, 2, 3, 4)
    key_pos = jnp.arange(s_)

    def one_block(args):
        qn_b, qr_b, n = args
        sc = (jnp.einsum('bqhd,bkhd->bhqk', qn_b, k_nope)
              + jnp.einsum('bqhd,bkd->bhqk', qr_b, k_r)).astype(jnp.float32) * scale
        qpos = n * BLOCK + jnp.arange(BLOCK)
        sc = jnp.where(key_pos[None, :] <= qpos[:, None], sc, NEG_INF)
        p = jax.nn.softmax(sc, axis=-1)
        return jnp.einsum('bhqk,bkhd->bqhd', p.astype(v.dtype), v)

    out = lax.map(one_block, (qn, qr, jnp.arange(nb)))
    return out.transpose(1, 0, 2, 3, 4).reshape(b_, s_, B_WIDTH)


def chunked_spatial_gating(u, v, ln_w, ln_b, w_s, b_s):
    v = layer_norm(v, ln_w, ln_b)
    b_, s_, _ = v.shape
    nc = s_ // C_CHUNK
    vc = v.reshape(b_, nc, C_CHUNK, C_GROUPS, C_GROUP_DIM)
    tri = jnp.tril(jnp.ones((C_CHUNK, C_CHUNK), dtype=bool))
    w = jnp.where(tri[None], w_s, jnp.zeros_like(w_s))
    mixed = jnp.einsum('gts,bnsgd->bntgd', w, vc) + b_s.T[None, None, :, :, None]
    return u * mixed.reshape(b_, s_, C_WIDTH)


def setup_inputs(seed: int = 0) -> dict:
    key = jax.random.key(seed)
    ks = jax.random.split(key, 24)
    nrm = lambda k, shape, s: jax.random.normal(k, shape, dtype=jnp.float32) * s
    gain = lambda k, shape: 1.0 + nrm(k, shape, 0.02)
    x = nrm(ks[0], (BATCH, SEQ, D_MODEL), 1.0)
    c = nrm(ks[1], (BATCH, D_MODEL), 1.0)
    offs = jax.random.randint(ks[2], (BATCH,), 0, 1024, dtype=jnp.int32)
    positions = offs[:, None] + jnp.arange(SEQ, dtype=jnp.int32)[None, :]
    return {
        "x": x,
        "c": c,
        "positions": positions,
        "ada_w": nrm(ks[3], (DEPTH, D_MODEL, N_MOD * D_MODEL), 0.5 * D_MODEL ** -0.5),
        "ada_b": nrm(ks[4], (DEPTH, N_MOD * D_MODEL), 0.02),
        "norm1_w": gain(ks[5], (DEPTH, D_MODEL)),
        "w_in": nrm(ks[6], (DEPTH, D_MODEL, IN_COLS), D_MODEL ** -0.5),
        "a_sinks": nrm(ks[7], (DEPTH, A_Q_HEADS), 0.5),
        "b_q_norm_w": gain(ks[8], (DEPTH, B_Q_RANK)),
        "b_w_uq": nrm(ks[9], (DEPTH, B_Q_RANK, B_HEADS * (B_NOPE + B_ROPE)), B_Q_RANK ** -0.5),
        "b_kv_norm_w": gain(ks[10], (DEPTH, B_KV_RANK)),
        "b_w_ukv": nrm(ks[11], (DEPTH, B_KV_RANK, B_HEADS * (B_NOPE + B_V)), B_KV_RANK ** -0.5),
        "c_ln_w": gain(ks[12], (DEPTH, C_WIDTH)),
        "c_ln_b": nrm(ks[13], (DEPTH, C_WIDTH), 0.02),
        "c_w_s": nrm(ks[14], (DEPTH, C_GROUPS, C_CHUNK, C_CHUNK), C_CHUNK ** -0.5),
        "c_b_s": gain(ks[15], (DEPTH, C_GROUPS, C_CHUNK)),
        "out_norm_w": gain(ks[16], (DEPTH, MIX_WIDTH)),
        "w_out": nrm(ks[17], (DEPTH, MIX_WIDTH, D_MODEL), MIX_WIDTH ** -0.5),
        "norm2_w": gain(ks[18], (DEPTH, D_MODEL)),
        "w_gate_up": nrm(ks[19], (DEPTH, D_MODEL, 2 * FFN_HIDDEN), D_MODEL ** -0.5),
        "w_down": nrm(ks[20], (DEPTH, FFN_HIDDEN, D_MODEL), FFN_HIDDEN ** -0.5),
        "final_norm_w": gain(ks[21], (D_MODEL,)),
    }


def reference(x, c, positions, ada_w, ada_b, norm1_w, w_in, a_sinks, b_q_norm_w, b_w_uq,
              b_kv_norm_w, b_w_ukv, c_ln_w, c_ln_b, c_w_s, c_b_s, out_norm_w, w_out,
              norm2_w, w_gate_up, w_down, final_norm_w):
    b_, s_, _ = x.shape
    cos_a, sin_a = rope_tables(positions, HEAD_DIM)
    cos_b, sin_b = rope_tables(positions, B_ROPE)
    split_at = np.cumsum(IN_SIZES)[:-1].tolist()
    c_act = jax.nn.silu(c)
    for l in range(DEPTH):
        mod = (c_act @ ada_w[l] + ada_b[l])[:, None, :]
        sh1, sc1, g1, sh2, sc2, g2 = jnp.split(mod, N_MOD, axis=-1)

        h = rms_norm(x, norm1_w[l]) * (1.0 + sc1) + sh1
        proj = h @ w_in[l]
        a_q, a_k, a_v, b_cq, b_ckv, b_kr, c_u, c_v = jnp.split(proj, split_at, axis=-1)

        qa = apply_rope(a_q.reshape(b_, s_, A_Q_HEADS, HEAD_DIM), cos_a, sin_a)
        ka = apply_rope(a_k.reshape(b_, s_, A_KV_HEADS, HEAD_DIM), cos_a, sin_a)
        va = a_v.reshape(b_, s_, A_KV_HEADS, HEAD_DIM)
        y_a = sliding_window_gqa(qa, ka, va, a_sinks[l])

        y_b = mla(b_cq, b_ckv, b_kr, b_q_norm_w[l], b_w_uq[l], b_kv_norm_w[l], b_w_ukv[l],
                  cos_b, sin_b)

        y_c = chunked_spatial_gating(jax.nn.gelu(c_u, approximate=False),
                                     jax.nn.gelu(c_v, approximate=False),
                                     c_ln_w[l], c_ln_b[l], c_w_s[l], c_b_s[l])

        gw = out_norm_w[l]
        y = jnp.concatenate([
            rms_norm(y_a, gw[:A_WIDTH]),
            rms_norm(y_b, gw[A_WIDTH:A_WIDTH + B_WIDTH]),
            rms_norm(y_c, gw[A_WIDTH + B_WIDTH:]),
        ], axis=-1)
        x = x + g1 * (y @ w_out[l])

        h = rms_norm(x, norm2_w[l]) * (1.0 + sc2) + sh2
        gate, up = jnp.split(h @ w_gate_up[l], 2, axis=-1)
        x = x + g2 * ((jax.nn.silu(gate) * up) @ w_down[l])
    return rms_norm(x, final_norm_w)
```

```python
import math
from contextlib import ExitStack
import numpy as np
import concourse.bass as bass
import concourse.mybir as mybir
from concourse.bass_utils import run_bass_kernel_spmd

F32 = mybir.dt.float32; BF16 = mybir.dt.bfloat16; I32 = mybir.dt.int32
AF = mybir.ActivationFunctionType; ALU = mybir.AluOpType; AX = mybir.AxisListType

class Buf:
    __slots__ = ("name", "w", "rs")
    def __init__(self, name=""):
        self.name = name; self.w = None; self.rs = []

class Ins:
    __slots__ = ("eng", "idx", "fn", "deps", "dma", "sig", "semi", "val", "tag", "name")
    def __init__(self, eng, idx, fn, dma):
        self.eng = eng; self.idx = idx; self.fn = fn; self.dma = dma
        self.deps = []; self.sig = False; self.semi = None; self.val = None; self.tag = None; self.name = None

ENGS = ("pe", "act", "dve", "pool", "sp")
NDMASEM = 8

class Sched:
    def __init__(self):
        self.prog = {e: [] for e in ENGS}
        self.dmas = {e: [] for e in ENGS}
        self.tag = ""
    def add(self, eng, fn, reads=(), writes=(), dma=False):
        lst = self.prog[eng]
        ins = Ins(eng, len(lst), fn, dma)
        ins.tag = self.tag
        deps = {}
        def dep(p):
            if p is None or p is ins: return
            if not p.dma and p.eng == eng:
                if eng == "pe": return
            deps[id(p)] = p
        for r in reads: dep(r.w)
        for w in writes:
            dep(w.w)
            for q in w.rs: dep(q)
        for r in reads: r.rs.append(ins)
        for w in writes:
            w.w = ins; w.rs = []
        if dma:
            dl = self.dmas[eng]; k = len(dl)
            ins.semi = k % NDMASEM; ins.val = 16 * (k // NDMASEM + 1)
            ins.sig = True
            if k >= NDMASEM:
                prev = dl[k - NDMASEM]
                deps[id(prev)] = prev
            dl.append(ins)
        ins.deps = list(deps.values())
        for p in ins.deps: p.sig = True
        lst.append(ins)
        return ins
    def pe(self, fn, reads=(), writes=()): return self.add("pe", fn, reads, writes)
    def act(self, fn, reads=(), writes=()): return self.add("act", fn, reads, writes)
    def dve(self, fn, reads=(), writes=()): return self.add("dve", fn, reads, writes)
    def pool(self, fn, reads=(), writes=()): return self.add("pool", fn, reads, writes)
    def dma(self, eng, fn, reads=(), writes=()): return self.add(eng, fn, reads, writes, dma=True)

    def emit(self, nc, final_waits=()):
        with ExitStack() as es:
            csem = {e: es.enter_context(nc.semaphore("c_" + e)) for e in ENGS}
            dsem = {e: [es.enter_context(nc.semaphore("d_%s%d" % (e, i))) for i in range(NDMASEM)]
                    for e in ENGS if len(self.dmas[e]) > 0}
            for e in ENGS:
                c = 0
                for ins in self.prog[e]:
                    if ins.dma: continue
                    if ins.sig:
                        c += 1; ins.val = c
            block = es.enter_context(nc.Block())
            def semof(p):
                return dsem[p.eng][p.semi] if p.dma else csem[p.eng]
            def body(ename, eng):
                waited = {}
                for ins in self.prog[ename]:
                    for p in ins.deps:
                        s = semof(p); key = (p.eng, p.semi if p.dma else -1)
                        if waited.get(key, 0) >= p.val: continue
                        eng.wait_ge(s, p.val); waited[key] = p.val
                    bi = ins.fn(eng)
                    ins.name = bi.ins.name
                    if ins.sig:
                        if ins.dma: bi.then_inc(dsem[ename][ins.semi], 16)
                        else: bi.then_inc(csem[ename], 1)
                if ename == "sp":
                    for p in final_waits:
                        eng.wait_ge(semof(p), p.val)
            @block.tensor
            def _(e): body("pe", e)
            @block.scalar
            def _(e): body("act", e)
            @block.vector
            def _(e): body("dve", e)
            @block.gpsimd
            def _(e): body("pool", e)
            @block.sync
            def _(e): body("sp", e)

def MM(out, lhsT, rhs, start=True, stop=True):
    return lambda e: e.matmul(out, lhsT=lhsT, rhs=rhs, start=start, stop=stop)
def TR(out, in_, ident):
    return lambda e: e.transpose(out, in_, ident)
def ACT(out, in_, func, bias=None, scale=None, accum_out=None):
    kw = {}
    if bias is not None: kw["bias"] = bias
    if scale is not None: kw["scale"] = scale
    if accum_out is not None: kw["accum_out"] = accum_out
    return lambda e: e.activation(out=out, in_=in_, func=func, **kw)
def TT(out, in0, in1, op):
    return lambda e: e.tensor_tensor(out=out, in0=in0, in1=in1, op=op)
def TS(out, in0, s1, s2, op0, op1=None):
    if op1 is None:
        return lambda e: e.tensor_scalar(out=out, in0=in0, scalar1=s1, scalar2=None, op0=op0)
    return lambda e: e.tensor_scalar(out=out, in0=in0, scalar1=s1, scalar2=s2, op0=op0, op1=op1)
def STT(out, in0, scalar, in1, op0, op1):
    return lambda e: e.scalar_tensor_tensor(out=out, in0=in0, scalar=scalar, in1=in1, op0=op0, op1=op1)
def CP(out, in_):
    return lambda e: e.tensor_copy(out=out, in_=in_)
def RCP(out, in_):
    return lambda e: e.reciprocal(out=out, in_=in_)
def RSUM(out, in_):
    return lambda e: e.reduce_sum(out=out, in_=in_, axis=AX.X)
def MSET(ap, v):
    return lambda e: e.memset(ap, v)
def DMA(out, in_):
    return lambda e: e.dma_start(out=out, in_=in_)

D = 1024; HID = 2816; NHC = 22; TT_ = 512; P = 128
EPS = 1e-6
WIN_C = 2368; WUQ_C = 960; WUKV_C = 1152; WGU_C = 5632; ADA_C = 6144
NV = 81
NBR = 518
NCON = 4 + 128 + 128 + 128
SLOT_ELEMS = 4096
NSLOT = 3

def build(L, NS, S, DEPTH_TOTAL=None):
    NT = S // TT_
    NB = S // 128
    nc = bass.Bass("TRN2", target_bir_lowering=False)
    din = lambda n, s, d: nc.dram_tensor(n, s, d, kind="ExternalInput").ap()
    xT = din("xT", [NS, D, S], F32)
    cT = din("cT", [P, 8, NS], F32)
    pos = din("pos", [NS, S], I32)
    win = din("win", [L, D, WIN_C], F32)
    wuq = din("wuq", [L, 384, WUQ_C], F32)
    wukv = din("wukv", [L, 256, WUKV_C], F32)
    wout = din("wout", [L, D, D], F32)
    wgu = din("wgu", [L, D, WGU_C], F32)
    wdn = din("wdn", [L, HID, D], F32)
    ada = din("ada", [L, D, ADA_C], F32)
    wsT = din("wsT", [L, P, 4, P], F32)
    pv = din("pv", [L, P, NV], F32)
    bro = din("bro", [L, NBR], F32)
    fnw = din("fnw", [P, 8], F32)
    con = din("con", [P, NCON], F32)
    outT = nc.dram_tensor("outT", [NS, D, S], F32, kind="ExternalOutput").ap()
    dsc = lambda n, s, d: nc.dram_tensor(n, s, d, kind="Internal").ap()
    win_b = dsc("win_b", [L, D, WIN_C], BF16)
    wuq_b = dsc("wuq_b", [L, 384, WUQ_C], BF16)
    wukv_b = dsc("wukv_b", [L, 256, WUKV_C], BF16)
    wout_b = dsc("wout_b", [L, D, D], BF16)
    wgu_b = dsc("wgu_b", [L, D, WGU_C], BF16)
    wdn_b = dsc("wdn_b", [L, HID, D], BF16)
    ada_b = dsc("ada_b", [L, D, ADA_C], BF16)
    xs = dsc("xs", [NS, D, S], F32)
    tabd = dsc("tabd", [NS, S // TT_, P, 4 * TT_], BF16)

    S_ = Sched()
    es = ExitStack()
    with es:
        sbt = lambda n, s, d: es.enter_context(nc.sbuf_tensor(n, s, d))
        ring = [sbt("ring%d" % i, [P, SLOT_ELEMS], BF16) for i in range(NSLOT)]
        ringB = [Buf("ring%d" % i) for i in range(NSLOT)]
        KT = sbt("KT", [P, 6, S], BF16); KTB = [Buf("KT%d" % t) for t in range(NT)]
        VA = sbt("VA", [P, NB, 3, 192], BF16); VAB = [Buf("VA%d" % t) for t in range(NT)]
        xt = sbt("xt", [P, 8, TT_], F32); xtB = [Buf("xt%d" % c) for c in range(8)]
        hb = sbt("hb", [P, 8, TT_], BF16); hbB = [Buf("hb%d" % c) for c in range(8)]
        yT = sbt("yT", [P, 8, TT_], BF16); yTB = [Buf("yT%d" % c) for c in range(8)]
        NHH = 11
        hid = sbt("hid", [P, NHH, TT_], BF16); hidB = [Buf("hid%d" % c) for c in range(NHH)]
        sqb = yT; sqbB = yTB
        QA = hid[:, 0:6, :]; QAB = hidB[0:6]
        cqb = hid[:, 6:9, :]; cqbB = hidB[6:9]
        ckvb = hid[:, 9:11, :]; ckvbB = hidB[9:11]
        sqq = yT[:, 0:3, :]; sqqB = yTB[0:3]
        sqkv = yT[:, 3:5, :]; sqkvB = yTB[3:5]
        Qa = yT[:, 5:8, :]; QaB = yTB[5:8]
        Ka = sbt("Ka", [P, 5 * 128], BF16); KaB = Buf("Ka")
        Va = sbt("Va", [P, 5, 192], BF16); VaB = Buf("Va")
        yraw = sbt("yraw", [P, 3, TT_], F32); yrawB = Buf("yraw")
        NPT = 5
        pts = [sbt("pt%d" % i, [P, TT_], BF16) for i in range(NPT)]; ptB = [Buf("pt%d" % i) for i in range(NPT)]
        NTMP = 4
        tmps = [sbt("tmp%d" % i, [P, TT_], F32) for i in range(NTMP)]; tmpB = [Buf("tmp%d" % i) for i in range(NTMP)]
        rstd1 = sbt("rstd1", [P, TT_], F32); rstd1B = Buf("rstd1")
        rstdq = sbt("rstdq", [P, TT_], F32); rstdqB = Buf("rstdq")
        rstdkv = sbt("rstdkv", [P, TT_], F32); rstdkvB = Buf("rstdkv")
        rkvc = sbt("rkvc", [P, 4], F32); rkvcB = Buf("rkvc")
        tabs = sbt("tabs", [P, 4, TT_], BF16); tabB = [Buf("tab%d" % i) for i in range(4)]
        posi = sbt("posi", [P, TT_], I32); posf = yraw[:, 0, :]; posB = Buf("pos")
        ki = posi; kiB = Buf("ki")
        guv = sbt("guv", [P, 4, 512], BF16); guvB = [Buf("guv%d" % j) for j in range(4)]
        vnb = sbt("vnb", [P, 4, 256], BF16); vnbB = [Buf("vnb%d" % j) for j in range(4)]
        ycn = sbt("ycn", [P, 4, 256], BF16); ycnB = [Buf("ycn%d" % j) for j in range(4)]
        dsa = sbt("dsa", [P, 3, TT_], F32); dsaB = Buf("dsa"); dsb = dsa[:, 0, :]; dsbB = dsaB
        st = sbt("st", [P, 32], F32); stB = Buf("st")
        pvt = sbt("pvt", [P, NV], F32); pvtB = Buf("pvt")
        brt = sbt("brt", [P, NBR], F32); brtB = Buf("brt")
        esk = sbt("esk", [P, 6], F32); eskB = Buf("esk")
        wst = sbt("wst", [P, 4, P], BF16); wstB = Buf("wst")
        modt = sbt("modt", [P, NS, 48], F32); modB = Buf("mod")
        modn = sbt("modn", [P, NS, 48], F32); modnB = Buf("modn")
        gn = sbt("gn", [P, NS, 16], F32); gnB = Buf("gn")
        cact = sbt("cact", [P, 8, NS], BF16); cf = sbt("cf", [P, 8, NS], F32); cactB = Buf("cact")
        cons = sbt("cons", [P, NCON], F32); consB = Buf("cons")
        fnwt = sbt("fnwt", [P, 8], F32)
        onesb = sbt("onesb", [P, P], BF16); identb = sbt("identb", [P, P], BF16); onesf = sbt("onesf", [P, P], F32)
        mD = sbt("mD", [P, 3, P], BF16); mP = sbt("mP", [P, 3, P], BF16)
        psb = [es.enter_context(nc.psum_tensor("psb%d" % i, [P, TT_], F32)) for i in range(7)]
        psB = [Buf("psb%d" % i) for i in range(7)]
        pst = es.enter_context(nc.psum_tensor("pst", [P, 1024], BF16)); pstB = Buf("pst")
        NGEN = 4
        state = {"g": 0, "tmp": 0, "pt": 0, "acc": 0, "pacc": 0}
        def bank():
            i = state["g"] % NGEN; state["g"] += 1
            return psb[i], psB[i]
        def accbank():
            i = NGEN + state["acc"] % 2; state["acc"] += 1
            return psb[i], psB[i]
        def tmp():
            i = state["tmp"] % NTMP; state["tmp"] += 1
            return tmps[i], tmpB[i]
        def ptile():
            i = state["pt"] % NPT; state["pt"] += 1
            return pts[i], ptB[i]
        denb, denbB = psb[6], psB[6]

        tabdB = [[Buf("tabd%d_%d" % (s, t)) for t in range(NT)] for s in range(NS)]
        xsB = [[[Buf("xs%d_%d_%d" % (s, t, c)) for c in range(8)] for t in range(NT)] for s in range(NS)]

        S_.dma("sp", DMA(cons[:], con[:, :]), [], [consB])
        S_.dma("sp", DMA(cf[:], cT[:, :, :]), [], [cactB])
        S_.dma("sp", DMA(fnwt[:], fnw[:, :]), [], [consB])
        S_.act(ACT(cact[:], cf[:], AF.Silu), [cactB], [cactB])
        S_.dve(MSET(onesb[:], 1.0), [], [consB])
        S_.dve(MSET(onesf[:], 1.0), [], [consB])
        S_.dve(CP(identb[:], cons[:, 260:388]), [consB], [consB])
        for r in range(3):
            S_.dve(TS(mD[:, r, :], cons[:, 4:132], 1.0, 30000.0, ALU.subtract, ALU.mult), [consB], [consB])
            S_.dve(TS(mP[:, r, :], cons[:, 132:260], 1.0, 30000.0, ALU.subtract, ALU.mult), [consB], [consB])
        S_.pool(MSET(KT[32:64, :, :], 0.0), [], KTB)
        S_.pool(MSET(VA[:, :, :, 64:128], 1.0), [], VAB)
        S_.pool(MSET(Va[:], 1.0), [], [VaB])
        invA = cons[:, 0:1]; invB = cons[:, 1:2]; sgnA = cons[:, 2:3]; sgnB = cons[:, 3:4]

        pieces = {}
        def cast_rows(name, src, dst, l, rows):
            lst = []
            r0 = 0
            while r0 < rows:
                r1 = min(rows, r0 + 128)
                pb_ = Buf()
                S_.dma("pool", DMA(dst[l, r0:r1, :], src[l, r0:r1, :]), [], [pb_])
                lst.append(pb_)
                r0 = r1
            pieces[(name, l)] = lst
        for l in range(L):
            cast_rows("ada", ada, ada_b, l, D)
            cast_rows("win", win, win_b, l, D)
            cast_rows("wuq", wuq, wuq_b, l, 384)
            cast_rows("wukv", wukv, wukv_b, l, 256)
            cast_rows("wout", wout, wout_b, l, D)
            cast_rows("wgu", wgu, wgu_b, l, D)
            cast_rows("wdn", wdn, wdn_b, l, HID)

        slabs = []
        def add_slab(name, l, src2d, k0, kcn, c0, c1):
            assert kcn * (c1 - c0) <= SLOT_ELEMS
            slabs.append((name, l, src2d, k0, kcn, c0, c1))
            return len(slabs) - 1
        loaded = {"n": 0}
        def issue_load(i):
            name, l, src2d, k0, kcn, c0, c1 = slabs[i]
            slot = i % NSLOT
            ncol = c1 - c0
            dst = ring[slot][:, 0:kcn * ncol].rearrange("p (k n) -> p k n", n=ncol)
            srcv = src2d[k0 * 128:(k0 + kcn) * 128, c0:c1].rearrange("(k p) n -> p k n", p=128)
            S_.dma("sp", DMA(dst, srcv), pieces[(name, l)], [ringB[slot]])
        def use_slab(i):
            while loaded["n"] < min(len(slabs), i + NSLOT):
                issue_load(loaded["n"]); loaded["n"] += 1
            name, l, src2d, k0, kcn, c0, c1 = slabs[i]
            slot = i % NSLOT
            ncol = c1 - c0
            return ring[slot][:, 0:kcn * ncol].rearrange("p (k n) -> p k n", n=ncol), ringB[slot]

        GU_GROUPS = [(0, 2), (2, 2), (4, 2), (6, 2), (8, 2), (10, 1), (11, 2), (13, 2), (15, 2), (17, 2), (19, 2), (21, 1)]
        NADA = 12
        plan = {}
        ada_slab = lambda l, i: add_slab("ada", l, ada_b[l], 0, 8, i * 512, (i + 1) * 512)
        for l in range(L):
            if l == 0:
                plan[("ada", 0)] = [ada_slab(0, i) for i in range(NADA)]
            for s in range(NS):
                for t in range(NT):
                    k = (l, s, t)
                    hoist = (l + 1 < L) and s == NS - 1 and t == NT - 1
                    if hoist: plan[("ada", l + 1)] = []
                    plan[("in0",) + k] = add_slab("win", l, win_b[l], 0, 8, 0, 512)
                    plan[("in0b",) + k] = add_slab("win", l, win_b[l], 0, 8, 512, 1024)
                    plan[("in1",) + k] = add_slab("win", l, win_b[l], 0, 8, 1024, 1536)
                    plan[("in1b",) + k] = add_slab("win", l, win_b[l], 0, 8, 1536, 1728)
                    plan[("in2",) + k] = add_slab("win", l, win_b[l], 0, 8, 1728, 2240)
                    plan[("in2b",) + k] = add_slab("win", l, win_b[l], 0, 8, 2240, 2368)
                    plan[("uq",) + k] = add_slab("wuq", l, wuq_b[l], 0, 3, 0, WUQ_C)
                    plan[("ukv",) + k] = add_slab("wukv", l, wukv_b[l], 0, 2, 0, WUKV_C)
                    plan[("out",) + k] = [add_slab("wout", l, wout_b[l], 0, 8, i * 512, (i + 1) * 512) for i in range(2)]
                    for hf in range(2):
                        lst = []
                        for gi, (h0, n) in enumerate(GU_GROUPS[hf * 6:(hf + 1) * 6]):
                            lst.append(add_slab("wgu", l, wgu_b[l], 0, 8, h0 * 256, (h0 + n) * 256))
                            if hoist: plan[("ada", l + 1)].append(ada_slab(l + 1, hf * 6 + gi))
                        plan[("gu", hf) + k] = lst
                        plan[("dn", hf) + k] = [add_slab("wdn", l, wdn_b[l], hf * 11, 11, i * 256, (i + 1) * 256) for i in range(4)]

        def rstd_from_ssq(ps_ap, psbuf, out_ap, outbuf, n, small=False):
            if small:
                S_.act(ACT(out_ap, ps_ap, AF.Sqrt, bias=EPS, scale=1.0 / n), [psbuf], [outbuf])
                S_.dve(RCP(out_ap, out_ap), [outbuf], [outbuf])
            else:
                S_.act(ACT(out_ap, ps_ap, AF.Ln, bias=EPS, scale=1.0 / n), [psbuf], [outbuf])
                S_.act(ACT(out_ap, out_ap, AF.Exp, scale=-0.5), [outbuf], [outbuf])

        def norm_mod(gcol, shcol, s):
            for c in range(8):
                S_.act(ACT(sqb[:, c, :], xt[:, c, :], AF.Square), [xtB[c]], [sqbB[c]])
            pb, pB = bank()
            for c in range(8):
                S_.pe(MM(pb[:], onesb[:], sqb[:, c, :], c == 0, c == 7), [sqbB[c], consB], [pB])
            rstd_from_ssq(pb[:], pB, rstd1[:], rstd1B, float(D))
            for c in range(8):
                tm, tB = tmp()
                S_.dve(TT(tm[:], xt[:, c, :], rstd1[:], ALU.mult), [xtB[c], rstd1B], [tB])
                S_.act(ACT(hb[:, c, :], tm[:], AF.Identity, bias=modt[:, s, shcol + c:shcol + c + 1],
                           scale=gn[:, s, gcol + c:gcol + c + 1]), [tB, modB, gnB], [hbB[c]])

        def proj_fm(slab, sB, col0, ncols, rhs_tile, rhsB, nk):
            pb, pB = bank()
            for k in range(nk):
                S_.pe(MM(pb[0:ncols, :], slab[:, k, col0:col0 + ncols], rhs_tile[:, k, :], k == 0, k == nk - 1),
                      [sB, rhsB[k]], [pB])
            return pb, pB

        def rope_tables(s, tok0):
            S_.dma("pool", DMA(posi[:], pos[s, tok0:tok0 + TT_].partition_broadcast(P)), [], [posB, kiB])
            S_.dve(CP(posf, posi[:]), [posB], [yrawB])
            for ti, (inv, sg, addq) in enumerate(((invA, None, 0.25), (invA, sgnA, 0.0), (invB, None, 0.25), (invB, sgnB, 0.0))):
                tm, tB = tmp()
                S_.dve(TS(tm[:], posf, inv, None, ALU.mult), [yrawB, consB], [tB])
                if addq != 0.0:
                    S_.dve(TS(tm[:], tm[:], addq, None, ALU.add), [tB], [tB])
                S_.dve(CP(ki[:], tm[:]), [tB, posB], [kiB])
                tm2, tB2 = tmp()
                S_.dve(CP(tm2[:], ki[:]), [kiB], [tB2])
                S_.dve(TT(tm[:], tm[:], tm2[:], ALU.subtract), [tB, tB2], [tB])
                S_.act(ACT(tabs[:, ti, :], tm[:], AF.Sin, scale=(sg if sg is not None else 2.0 * math.pi)),
                       [tB, consB], [tabB[ti]])

        def rope_combine(pa, paB, pr, prB, rows, cos_ap, sin_ap, cB, out_ap, outB, post=None, postB=None):
            t1, t1B = tmp()
            S_.dve(TT(t1[rows, :], pa[rows, :], cos_ap, ALU.mult), [paB] + cB, [t1B])
            t2, t2B = tmp()
            S_.dve(TT(t2[rows, :], pr[rows, :], sin_ap, ALU.mult), [prB] + cB, [t2B])
            if post is None:
                S_.pool(TT(out_ap, t1[rows, :], t2[rows, :], ALU.add), [t1B, t2B], outB)
            else:
                S_.pool(TT(t1[rows, :], t1[rows, :], t2[rows, :], ALU.add), [t1B, t2B], [t1B])
                S_.pool(TT(out_ap, t1[rows, :], post, ALU.mult), [t1B, postB], outB)

        def group_norm_to_y(n_feat, ychunk0):
            S_.pool(TT(hb[:, 0:3, :], yraw[:], yraw[:], ALU.mult), [yrawB], hbB[0:3])
            pb, pB = bank()
            for c in range(3):
                S_.pe(MM(pb[:], onesb[:], hb[:, c, :], c == 0, c == 2), [hbB[c], consB], [pB])
            tm, tB = tmp()
            rstd_from_ssq(pb[:], pB, tm[:], tB, float(n_feat))
            for j in range(3):
                S_.dve(STT(yT[:, ychunk0 + j, :], yraw[:, j, :], pvt[:, 21 + ychunk0 + j:22 + ychunk0 + j], tm[:],
                           ALU.mult, ALU.mult), [yrawB, pvtB, tB], [yTB[ychunk0 + j]])

        def emit_ada(l_, i):
            slab, sB = use_slab(plan[("ada", l_)][i])
            pb, pB = bank()
            for c in range(4):
                for k in range(8):
                    S_.pe(MM(pb[:, c * NS:(c + 1) * NS], slab[:, k, c * 128:(c + 1) * 128], cact[:, k, :], k == 0, k == 7),
                          [sB, cactB], [pB])
            for s_ in range(NS):
                S_.dve(CP(modn[:, s_, i * 4:(i + 1) * 4], pb[:, 0:4 * NS].rearrange("p (m s) -> p s m", s=NS)[:, s_, :]), [pB], [modnB])

        final_dmas = []
        for l in range(L):
            last_layer = (l == L - 1)
            S_.tag = "prologue"
            S_.dma("pool", DMA(pvt[:], pv[l, :, :]), [], [pvtB])
            S_.dma("pool", DMA(brt[:], bro[l, :].partition_broadcast(P)), [], [brtB])
            tmw, tmwB = tmp()
            S_.dma("pool", DMA(tmw[:].rearrange("p (g t) -> p g t", t=P), wsT[l, :, :, :]), [], [tmwB])
            for g in range(4):
                S_.dve(TT(wst[:, g, :], tmw[:, g * P:(g + 1) * P], cons[:, 4:132], ALU.mult), [tmwB, consB], [wstB])
            S_.act(ACT(esk[:], brt[:, 0:6], AF.Exp), [brtB], [eskB])
            if l == 0:
                for i in range(NADA):
                    emit_ada(0, i)
            for s in range(NS):
                S_.dve(TT(modt[:, s, :], modn[:, s, :], pvt[:, 29:77], ALU.add), [modnB, pvtB], [modB])
                S_.dve(STT(gn[:, s, 0:8], modt[:, s, 8:16], 1.0, pvt[:, 0:8], ALU.add, ALU.mult), [modB, pvtB], [gnB])
                S_.dve(STT(gn[:, s, 8:16], modt[:, s, 32:40], 1.0, pvt[:, 8:16], ALU.add, ALU.mult), [modB, pvtB], [gnB])

            for s in range(NS):
                for t in range(NT):
                    key = (l, s, t)
                    tok0 = t * TT_
                    src = xT if l == 0 else xs
                    if l > 0:
                        S_.dma("pool", DMA(tabs[:].rearrange("p a b -> p (a b)"), tabd[s, t, :, :]), [tabdB[s][t]], tabB)
                    for c in range(8):
                        S_.dma("pool", DMA(xt[:, c, :], src[s, c * P:(c + 1) * P, tok0:tok0 + TT_]), [xsB[s][t][c]], [xtB[c]])
                    cosA = tabs[:, 0, :]; sinA = tabs[:, 1, :]
                    S_.tag = "norm1"
                    norm_mod(0, 0, s)
                    S_.tag = "load_rope"
                    if l == 0:
                        rope_tables(s, tok0)
                        if L > 1:
                            S_.dma("pool", DMA(tabd[s, t, :, :], tabs[:].rearrange("p a b -> p (a b)")), tabB, [tabdB[s][t]])
                    S_.tag = "win_a"
                    sl0, sB0 = use_slab(plan[("in0",) + key])
                    if t > 0:
                        S_.pool(CP(Ka[:, 0:128], Ka[:, 512:640]), [KaB], [KaB])
                        S_.pool(CP(Va[:, 0, :], Va[:, 4, :]), [VaB], [VaB])
                    for j in range(2):
                        pa, paB = proj_fm(sl0, sB0, (2 * j) * 128, 128, hb, hbB, 8)
                        pr, prB = proj_fm(sl0, sB0, (2 * j + 1) * 128, 128, hb, hbB, 8)
                        rope_combine(pa, paB, pr, prB, slice(0, P), cosA, sinA, [tabB[0], tabB[1]], Qa[:, j, :], [QaB[j]])
                    sl0b, sB0b = use_slab(plan[("in0b",) + key])
                    pa, paB = proj_fm(sl0b, sB0b, 0, 128, hb, hbB, 8)
                    pr, prB = proj_fm(sl0b, sB0b, 128, 128, hb, hbB, 8)
                    rope_combine(pa, paB, pr, prB, slice(0, P), cosA, sinA, [tabB[0], tabB[1]], Qa[:, 2, :], [QaB[2]])
                    pa, paB = proj_fm(sl0b, sB0b, 256, 128, hb, hbB, 8)
                    pr, prB = proj_fm(sl0b, sB0b, 384, 128, hb, hbB, 8)
                    rope_combine(pa, paB, pr, prB, slice(0, P), cosA, sinA, [tabB[0], tabB[1]], Ka[:, 128:640], [KaB])
                    S_.tag = "win_b"
                    sl1, sB1 = use_slab(plan[("in1",) + key])
                    for j in range(3):
                        pa, paB = proj_fm(sl1, sB1, j * 128, 128, hb, hbB, 8)
                        S_.act(ACT(cqb[:, j, :], pa[:], AF.Copy, scale=pvt[:, 16 + j:17 + j]), [paB, pvtB], [cqbB[j]])
                        S_.act(ACT(sqq[:, j, :], pa[:], AF.Square), [paB], [sqqB[j]])
                    sl1b, sB1b = None, None
                    for j in range(2):
                        if j == 0:
                            pa, paB = proj_fm(sl1, sB1, 384, 128, hb, hbB, 8)
                        else:
                            sl1b, sB1b = use_slab(plan[("in1b",) + key])
                            pa, paB = proj_fm(sl1b, sB1b, 0, 128, hb, hbB, 8)
                        S_.act(ACT(ckvb[:, j, :], pa[:], AF.Copy, scale=pvt[:, 19 + j:20 + j]), [paB, pvtB], [ckvbB[j]])
                        S_.act(ACT(sqkv[:, j, :], pa[:], AF.Square), [paB], [sqkvB[j]])
                    pb, pB = bank()
                    for c in range(3):
                        S_.pe(MM(pb[:], onesb[:], sqq[:, c, :], c == 0, c == 2), [sqqB[c], consB], [pB])
                    rstd_from_ssq(pb[:], pB, rstdq[:], rstdqB, 384.0)
                    pb, pB = bank()
                    for c in range(2):
                        S_.pe(MM(pb[:], onesb[:], sqkv[:, c, :], c == 0, c == 1), [sqkvB[c], consB], [pB])
                    rstd_from_ssq(pb[:], pB, rstdkv[:], rstdkvB, 256.0)
                    pb, pB = bank()
                    for j in range(4):
                        for c in range(2):
                            S_.pe(MM(pb[:, j:j + 1], sqkv[:, c, j * 128:(j + 1) * 128], onesb[:, 0:1], c == 0, c == 1),
                                  [sqkvB[c], consB], [pB])
                    rstd_from_ssq(pb[:, 0:4], pB, rkvc[:], rkvcB, 256.0, small=True)
                    cosB = tabs[0:32, 2, :]; sinB = tabs[0:32, 3, :]
                    pa, paB = proj_fm(sl1b, sB1b, 128, 32, hb, hbB, 8)
                    pr, prB = proj_fm(sl1b, sB1b, 160, 32, hb, hbB, 8)
                    rope_combine(pa, paB, pr, prB, slice(0, 32), cosB, sinB, [tabB[2], tabB[3]],
                                 KT[0:32, 0, tok0:tok0 + TT_], [KTB[t]])
                    for h in range(1, 6):
                        S_.pool(CP(KT[0:32, h, tok0:tok0 + TT_], KT[0:32, 0, tok0:tok0 + TT_]), [KTB[t]], [KTB[t]])
                    S_.tag = "win_tok"
                    sl2, sB2 = use_slab(plan[("in2",) + key])
                    for j in range(4):
                        pb, pB = bank()
                        for k in range(8):
                            S_.pe(MM(pb[:], hb[:, k, j * 128:(j + 1) * 128], sl2[:, k, 0:512], k == 0, k == 7), [hbB[k], sB2], [pB])
                        S_.act(ACT(guv[:, j, :], pb[:], AF.Gelu), [pB], [guvB[j]])
                    sl2b, sB2b = use_slab(plan[("in2b",) + key])
                    for j in range(4):
                        pb2, pB2 = bank()
                        for k in range(8):
                            S_.pe(MM(pb2[:, 0:128], hb[:, k, j * 128:(j + 1) * 128], sl2b[:, k, 0:128], k == 0, k == 7), [hbB[k], sB2b], [pB2])
                        S_.dve(CP(Va[:, 1 + j, :].rearrange("p (a b) -> p a b", b=64)[:, 0::2, :],
                                  pb2[:, 0:128].rearrange("p (a b) -> p a b", b=64)), [pB2], [VaB])
                    S_.tag = "mla_up"
                    slq, sBq = use_slab(plan[("uq",) + key])
                    for h in range(6):
                        pa, paB = proj_fm(slq, sBq, h * 128, 128, cqb, cqbB, 3)
                        pr, prB = proj_fm(slq, sBq, 768 + h * 32, 32, cqb, cqbB, 3)
                        S_.dve(TT(QA[:, h, :], pa[:], rstdq[:], ALU.mult), [paB, rstdqB], [QAB[h]])
                        rope_combine(pa, paB, pr, prB, slice(0, 32), cosB, sinB, [tabB[2], tabB[3]],
                                     QA[0:32, h, :], [QAB[h]], post=rstdq[0:32, :], postB=rstdqB)
                    slk, sBk = use_slab(plan[("ukv",) + key])
                    for h in range(6):
                        pa, paB = proj_fm(slk, sBk, h * 128, 128, ckvb, ckvbB, 2)
                        S_.dve(TT(KT[64:128, h, tok0:tok0 + TT_], pa[64:128, :], rstdkv[64:128, :], ALU.mult),
                               [paB, rstdkvB], [KTB[t]])
                    for j in range(4):
                        pb, pB = bank()
                        for c in range(2):
                            S_.pe(MM(pb[:, 0:384], ckvb[:, c, j * 128:(j + 1) * 128], slk[:, c, 768:1152], c == 0, c == 1),
                                  [ckvbB[c], sBk], [pB])
                        bi = 4 * t + j
                        S_.act(ACT(VA[:, bi, :, 0:64], pb[:, 0:192].rearrange("p (a b) -> p a b", b=64), AF.Copy,
                                   scale=rkvc[:, j:j + 1]), [pB, rkvcB], [VAB[t]])
                        S_.act(ACT(VA[:, bi, :, 128:192], pb[:, 192:384].rearrange("p (a b) -> p a b", b=64), AF.Copy,
                                   scale=rkvc[:, j:j + 1]), [pB, rkvcB], [VAB[t]])

                    S_.tag = "sgu1"
                    S_.dve(MSET(st[:], 0.0), [], [stB])
                    for j in range(4):
                        S_.dve(RSUM(st[:, j:j + 1], guv[:, j, 256:512]), [guvB[j]], [stB])
                        tm, tB = tmp()
                        S_.act(ACT(tm[:, 0:256], guv[:, j, 256:512], AF.Square, accum_out=st[:, 4 + j:5 + j]), [guvB[j], stB], [tB, stB])
                    S_.dve(TS(st[:, 8:12], st[:, 0:4], 1.0 / 256, None, ALU.mult), [stB], [stB])
                    S_.dve(TT(st[:, 12:16], st[:, 8:12], st[:, 8:12], ALU.mult), [stB], [stB])
                    S_.dve(STT(st[:, 16:20], st[:, 4:8], 1.0 / 256, st[:, 12:16], ALU.mult, ALU.subtract), [stB], [stB])
                    S_.act(ACT(st[:, 16:20], st[:, 16:20], AF.Sqrt, bias=EPS), [stB], [stB])
                    S_.dve(RCP(st[:, 16:20], st[:, 16:20]), [stB], [stB])
                    for j in range(4):
                        tm, tB = tmp()
                        S_.dve(TS(tm[:, 0:256], guv[:, j, 256:512], st[:, 8 + j:9 + j], st[:, 16 + j:17 + j], ALU.subtract, ALU.mult),
                               [guvB[j], stB], [tB])
                        S_.pool(TT(tm[:, 0:256], tm[:, 0:256], brt[:, 6:262], ALU.mult), [tB, brtB], [tB])
                        S_.pool(TT(vnb[:, j, :], tm[:, 0:256], brt[:, 262:518], ALU.add), [tB, brtB], [vnbB[j]])

                    S_.tag = "swa"
                    iters = [(b, kv) for b in range(4) for kv in range(2)]
                    def swa_stage_a(b, kv):
                        gblk = 4 * t + b
                        rows = slice(kv * 64, (kv + 1) * 64)
                        kts = ([] if gblk == 0 else [(b, mP)]) + [(b + 1, mD)]
                        outl = []
                        for (blk, msk) in kts:
                            pb, pB = bank()
                            S_.pe(MM(pb[:, 0:384], Ka[rows, blk * 128:(blk + 1) * 128],
                                     Qa[rows, :, b * 128:(b + 1) * 128], True, False), [KaB] + QaB, [pB])
                            S_.pe(MM(pb[:, 0:384], identb[:], msk[:].rearrange("p a b -> p (a b)"), False, True), [consB], [pB])
                            pt, pB2 = ptile()
                            S_.act(ACT(pt[:, 0:384], pb[:, 0:384], AF.Exp, scale=0.125), [pB], [pB2])
                            outl.append((blk, pt, pB2))
                        return outl
                    def swa_stage_b(b, kv, pl):
                        rows = slice(kv * 64, (kv + 1) * 64)
                        drows = slice((1 - kv) * 64, (2 - kv) * 64)
                        acc, accB = accbank()
                        for ii, (blk, pt, pB2) in enumerate(pl):
                            vsl = Va[:, blk, 0:128] if kv == 0 else Va[:, blk, 64:192]
                            S_.pe(MM(acc[:, 0:384], vsl, pt[:, 0:384], ii == 0, ii == len(pl) - 1), [VaB, pB2], [accB])
                        for hh in range(3):
                            h = 3 * kv + hh
                            S_.dve(TS(dsa[rows, hh, b * 128:(b + 1) * 128], acc[drows, hh * 128:(hh + 1) * 128], esk[rows, h:h + 1], None, ALU.add),
                                   [accB, eskB], [dsaB])
                        S_.dve(CP(yraw[rows, :, b * 128:(b + 1) * 128], acc[rows, 0:384].rearrange("p (a b) -> p a b", b=128)), [accB], [yrawB])
                    pend = swa_stage_a(*iters[0])
                    for ii_ in range(len(iters)):
                        nxt = swa_stage_a(*iters[ii_ + 1]) if ii_ + 1 < len(iters) else None
                        swa_stage_b(iters[ii_][0], iters[ii_][1], pend)
                        pend = nxt
                    S_.act(ACT(dsa[:], dsa[:], AF.Ln), [dsaB], [dsaB])
                    S_.act(ACT(dsa[:], dsa[:], AF.Exp, scale=-1.0), [dsaB], [dsaB])
                    S_.dve(TT(yraw[:], yraw[:], dsa[:], ALU.mult), [yrawB, dsaB], [yrawB])
                    group_norm_to_y(384, 0)

                    S_.tag = "sgu2"
                    for j in range(4):
                        pb, pB = bank()
                        for g in range(4):
                            S_.pe(MM(pb[:, g * 64:(g + 1) * 64], wst[:, g, :], vnb[:, j, g * 64:(g + 1) * 64]), [wstB, vnbB[j]], [pB])
                        ycj, ycjB = tmp()
                        for g in range(4):
                            S_.dve(STT(ycj[:, g * 64:(g + 1) * 64], pb[:, g * 64:(g + 1) * 64], pvt[:, 77 + g:78 + g],
                                       guv[:, j, g * 64:(g + 1) * 64], ALU.add, ALU.mult), [pB, pvtB, guvB[j]], [ycjB])
                        S_.act(ACT(ycj[:, 256:512], ycj[:, 0:256], AF.Square, accum_out=st[:, 20 + j:21 + j]), [ycjB, stB], [ycjB, stB])
                        S_.act(ACT(st[:, 24 + j:25 + j], st[:, 20 + j:21 + j], AF.Sqrt, bias=EPS, scale=1.0 / 256), [stB], [stB])
                        S_.dve(RCP(st[:, 24 + j:25 + j], st[:, 24 + j:25 + j]), [stB], [stB])
                        S_.act(ACT(ycn[:, j, :], ycj[:, 0:256], AF.Copy, scale=st[:, 24 + j:25 + j]), [ycjB, stB], [ycnB[j]])

                    S_.tag = "mla"
                    LA = 3
                    deferred = [None]
                    for h in range(6):
                        acc, accB = accbank()
                        nkt = 4 * (t + 1)
                        def mla_a(kt):
                            d = kt - 4 * t
                            q0 = 0 if d < 0 else d * 128
                            pb, pB = bank()
                            S_.pe(MM(pb[:, q0:TT_], KT[:, h, kt * 128:(kt + 1) * 128], QA[:, h, q0:TT_], True, d < 0),
                                  [KTB[kt // 4], QAB[h]], [pB])
                            if d >= 0:
                                S_.pe(MM(pb[:, q0:q0 + 128], identb[:], mD[:, 0, :], False, True), [consB], [pB])
                            pt, pB2 = ptile()
                            S_.act(ACT(pt[:, q0:TT_], pb[:, q0:TT_], AF.Exp, scale=96.0 ** -0.5), [pB], [pB2])
                            return (q0, pt, pB2)
                        def mla_b(kt, q0, pt, pB2):
                            vsl = VA[:, kt, h, 0:128] if h < 3 else VA[:, kt, h - 3, 64:192]
                            S_.pe(MM(acc[:, q0:TT_], vsl, pt[:, q0:TT_], kt == 0, kt == nkt - 1),
                                  [VAB[kt // 4], pB2], [accB])
                        pendq = []; nb_ = [0]
                        def do_b():
                            mla_b(*pendq.pop(0)); nb_[0] += 1
                            if nb_[0] == 1 and deferred[0] is not None:
                                deferred[0](); deferred[0] = None
                        for kt in range(nkt):
                            pendq.append((kt,) + mla_a(kt))
                            if len(pendq) > LA:
                                do_b()
                        while pendq:
                            do_b()
                        rows = slice(0, 64) if h < 3 else slice(64, 128)
                        drows = slice(64, 128) if h < 3 else slice(0, 64)
                        def mk_finish(rows, hh, acc, accB, drows=drows):
                            def fin():
                                S_.act(ACT(dsb[rows, :], acc[drows, :], AF.Ln), [accB], [dsbB])
                                S_.act(ACT(dsb[rows, :], dsb[rows, :], AF.Exp, scale=-1.0), [dsbB], [dsbB])
                                S_.dve(TT(yraw[rows, hh, :], acc[rows, :], dsb[rows, :], ALU.mult), [accB, dsbB], [yrawB])
                            return fin
                        deferred[0] = mk_finish(rows, h % 3, acc, accB)
                    deferred[0](); deferred[0] = None
                    group_norm_to_y(384, 3)

                    S_.tag = "sgu3"
                    for j in range(4):
                        for c in range(2):
                            S_.pe(TR(pst[:, (c * 4 + j) * 128:(c * 4 + j + 1) * 128], ycn[:, j, c * 128:(c + 1) * 128], identb[:]),
                                  [ycnB[j], consB], [pstB])
                    for c in range(2):
                        S_.act(ACT(yT[:, 6 + c, :], pst[:, c * 512:(c + 1) * 512], AF.Copy, scale=pvt[:, 27 + c:28 + c]),
                               [pstB, pvtB], [yTB[6 + c]])

                    S_.tag = "outproj"
                    for i in range(2):
                        slo, sBo = use_slab(plan[("out",) + key][i])
                        for q in range(4):
                            fc = i * 4 + q
                            pb, pB = bank()
                            for k in range(8):
                                S_.pe(MM(pb[:], slo[:, k, q * 128:(q + 1) * 128], yT[:, k, :], k == 0, k == 7), [sBo, yTB[k]], [pB])
                            S_.dve(STT(xt[:, fc, :], pb[:], modt[:, s, 16 + fc:17 + fc], xt[:, fc, :], ALU.mult, ALU.add),
                                   [pB, modB, xtB[fc]], [xtB[fc]])

                    S_.tag = "ffn"
                    norm_mod(8, 24, s)
                    for hf in range(2):
                        for gi, (h0, n) in enumerate(GU_GROUPS[hf * 6:(hf + 1) * 6]):
                            slg, sBg = use_slab(plan[("gu", hf) + key][gi])
                            for q in range(n):
                                hl = h0 + q - hf * 11
                                pg, pgB = bank()
                                for k in range(8):
                                    S_.pe(MM(pg[:], slg[:, k, q * 256:q * 256 + 128], hb[:, k, :], k == 0, k == 7), [sBg, hbB[k]], [pgB])
                                pu, puB = bank()
                                for k in range(8):
                                    S_.pe(MM(pu[:], slg[:, k, q * 256 + 128:q * 256 + 256], hb[:, k, :], k == 0, k == 7), [sBg, hbB[k]], [puB])
                                tm, tB = tmp()
                                S_.act(ACT(tm[:], pg[:], AF.Silu), [pgB], [tB])
                                S_.dve(TT(hid[:, hl, :], pu[:], tm[:], ALU.mult), [puB, tB], [hidB[hl]])
                            if (l + 1 < L) and s == NS - 1 and t == NT - 1:
                                S_.tag = "ada"; emit_ada(l + 1, hf * 6 + gi); S_.tag = "ffn"
                        for i in range(4):
                            sld, sBd = use_slab(plan[("dn", hf) + key][i])
                            for q in range(2):
                                fc = i * 2 + q
                                pb, pB = bank()
                                for k in range(NHH):
                                    S_.pe(MM(pb[:], sld[:, k, q * 128:(q + 1) * 128], hid[:, k, :], k == 0, k == NHH - 1),
                                          [sBd, hidB[k]], [pB])
                                S_.dve(STT(xt[:, fc, :], pb[:], modt[:, s, 40 + fc:41 + fc], xt[:, fc, :], ALU.mult, ALU.add),
                                       [pB, modB, xtB[fc]], [xtB[fc]])
                    S_.tag = "store"
                    if last_layer:
                        for c in range(8):
                            S_.act(ACT(sqb[:, c, :], xt[:, c, :], AF.Square), [xtB[c]], [sqbB[c]])
                        pb, pB = bank()
                        for c in range(8):
                            S_.pe(MM(pb[:], onesb[:], sqb[:, c, :], c == 0, c == 7), [sqbB[c], consB], [pB])
                        rstd_from_ssq(pb[:], pB, rstd1[:], rstd1B, float(D))
                        for c in range(8):
                            S_.dve(STT(xt[:, c, :], xt[:, c, :], fnwt[:, c:c + 1], rstd1[:], ALU.mult, ALU.mult),
                                   [xtB[c], rstd1B, consB], [xtB[c]])
                            dm = S_.dma("pool", DMA(outT[s, c * P:(c + 1) * P, tok0:tok0 + TT_], xt[:, c, :]), [xtB[c]], [Buf()])
                            final_dmas.append(dm)
                    else:
                        for c in range(8):
                            S_.dma("pool", DMA(xs[s, c * P:(c + 1) * P, tok0:tok0 + TT_], xt[:, c, :]), [xtB[c]], [xsB[s][t][c]])
        S_.emit(nc, final_waits=final_dmas)
    return nc, S_

def _rot(cols, n):
    h = n // 2
    return np.concatenate([cols[..., h:], cols[..., :h]], axis=-1)

def prep_weights(inp, L):
    f = lambda a: np.ascontiguousarray(np.asarray(a, dtype=np.float32))
    w_in = f(inp["w_in"])
    win = np.zeros((L, D, WIN_C), np.float32)
    aq = lambda h: w_in[:, :, h * 64:(h + 1) * 64]
    ak = lambda k: w_in[:, :, 384 + k * 64:384 + (k + 1) * 64]
    for j in range(3):
        c0 = 2 * j * 128; c1 = (2 * j + 1) * 128
        win[:, :, c0:c0 + 64] = aq(j); win[:, :, c0 + 64:c0 + 128] = aq(j + 3)
        win[:, :, c1:c1 + 64] = _rot(aq(j), 64); win[:, :, c1 + 64:c1 + 128] = _rot(aq(j + 3), 64)
    win[:, :, 768:832] = ak(0); win[:, :, 832:896] = ak(1)
    win[:, :, 896:960] = _rot(ak(0), 64); win[:, :, 960:1024] = _rot(ak(1), 64)
    win[:, :, 1024:1408] = w_in[:, :, 640:1024]
    win[:, :, 1408:1664] = w_in[:, :, 1024:1280]
    win[:, :, 1664:1696] = w_in[:, :, 1280:1312]
    win[:, :, 1696:1728] = _rot(w_in[:, :, 1280:1312], 32)
    win[:, :, 1728:2240] = w_in[:, :, 1312:1824]
    win[:, :, 2240:2368] = w_in[:, :, 512:640]
    w_uq = f(inp["b_w_uq"])
    wuq = np.zeros((L, 384, WUQ_C), np.float32)
    for h in range(6):
        nope = w_uq[:, :, h * 96:h * 96 + 64]; rope = w_uq[:, :, h * 96 + 64:h * 96 + 96]
        wuq[:, :, h * 128:h * 128 + 32] = rope
        wuq[:, :, h * 128 + 64:h * 128 + 128] = nope
        wuq[:, :, 768 + h * 32:768 + (h + 1) * 32] = _rot(rope, 32)
    w_ukv = f(inp["b_w_ukv"])
    wukv = np.zeros((L, 256, WUKV_C), np.float32)
    for h in range(6):
        wukv[:, :, h * 128 + 64:h * 128 + 128] = w_ukv[:, :, h * 128:h * 128 + 64]
        wukv[:, :, 768 + h * 64:768 + (h + 1) * 64] = w_ukv[:, :, h * 128 + 64:h * 128 + 128]
    perm = []
    for base in (0, 384):
        for j in range(3):
            perm += list(range(base + j * 64, base + (j + 1) * 64)) + list(range(base + (j + 3) * 64, base + (j + 4) * 64))
    perm += list(range(768, 1024))
    perm = np.array(perm)
    wout = f(inp["w_out"])[:, perm, :]
    gw = f(inp["out_norm_w"])[:, perm]
    wgu_o = f(inp["w_gate_up"])
    wgu = np.zeros((L, D, WGU_C), np.float32)
    for hc in range(NHC):
        wgu[:, :, hc * 256:hc * 256 + 128] = wgu_o[:, :, hc * 128:(hc + 1) * 128]
        wgu[:, :, hc * 256 + 128:(hc + 1) * 256] = wgu_o[:, :, HID + hc * 128:HID + (hc + 1) * 128]
    wdn = f(inp["w_down"])
    ada = f(inp["ada_w"])
    wsT = np.ascontiguousarray(np.transpose(f(inp["c_w_s"]), (0, 3, 1, 2)))
    pm = lambda v, n: np.transpose(v.reshape(L, n, 128), (0, 2, 1))
    pv = np.zeros((L, P, NV), np.float32)
    pv[:, :, 0:8] = pm(f(inp["norm1_w"]), 8)
    pv[:, :, 8:16] = pm(f(inp["norm2_w"]), 8)
    pv[:, :, 16:19] = pm(f(inp["b_q_norm_w"]), 3)
    pv[:, :, 19:21] = pm(f(inp["b_kv_norm_w"]), 2)
    pv[:, :, 21:29] = pm(gw, 8)
    pv[:, :, 29:77] = pm(f(inp["ada_b"]), 48)
    pv[:, :, 77:81] = np.transpose(f(inp["c_b_s"]), (0, 2, 1))
    bro = np.concatenate([f(inp["a_sinks"]), f(inp["c_ln_w"]), f(inp["c_ln_b"])], axis=1)
    fnw = np.ascontiguousarray(f(inp["final_norm_w"]).reshape(8, 128).T)
    con = np.zeros((P, NCON), np.float32)
    p = np.arange(P)
    con[:, 0] = (10000.0 ** (-(2.0 * (p % 32)) / 64.0)) / (2 * np.pi)
    con[:, 1] = (10000.0 ** (-(2.0 * (p % 16)) / 32.0)) / (2 * np.pi)
    con[:, 2] = np.where((p % 64) < 32, -2 * np.pi, 2 * np.pi)
    con[:, 3] = np.where((p % 32) < 16, -2 * np.pi, 2 * np.pi)
    kk = p[:, None]; qq = p[None, :]
    con[:, 4:132] = (kk <= qq)
    con[:, 132:260] = (kk > qq)
    con[:, 260:388] = np.eye(P)
    return dict(win=win, wuq=wuq, wukv=wukv, wout=wout, wgu=wgu, wdn=wdn, ada=ada, wsT=wsT, pv=pv,
                bro=np.ascontiguousarray(bro), fnw=fnw, con=con)

def run(inp, L, NS, S, n_cores, trace=False):
    nc, _ = build(L, NS, S)
    wts = prep_weights(inp, L)
    x = np.asarray(inp["x"], np.float32); c = np.asarray(inp["c"], np.float32)
    posn = np.asarray(inp["positions"]).astype(np.int32)
    in_maps = []
    for i in range(n_cores):
        b0 = i * NS
        m = dict(wts)
        m["xT"] = np.ascontiguousarray(np.transpose(x[b0:b0 + NS], (0, 2, 1)))
        m["cT"] = np.ascontiguousarray(np.transpose(c[b0:b0 + NS].reshape(NS, 8, 128), (2, 1, 0)))
        m["pos"] = np.ascontiguousarray(posn[b0:b0 + NS])
        in_maps.append(m)
    res = run_bass_kernel_spmd(nc, in_maps, core_ids=list(range(n_cores)), **({"trace": True} if trace else {}))
    outs = [np.transpose(r["outT"], (0, 2, 1)) for r in res.results]
    return np.ascontiguousarray(np.concatenate(outs, axis=0)).astype(np.float32), res

def kernel(**inputs):
    out, _ = run(inputs, 4, 2, 4096, 8)
    return out
```

```python
import math
from contextlib import ExitStack
import numpy as np
import concourse.bass as bass
import concourse.mybir as mybir
from concourse.bass_utils import run_bass_kernel_spmd

F32 = mybir.dt.float32; BF16 = mybir.dt.bfloat16; I32 = mybir.dt.int32
AF = mybir.ActivationFunctionType; ALU = mybir.AluOpType; AX = mybir.AxisListType

class Buf:
    __slots__ = ("name", "w", "rs")
    def __init__(self, name=""):
        self.name = name; self.w = None; self.rs = []

class Ins:
    __slots__ = ("eng", "idx", "fn", "deps", "dma", "sig", "semi", "val", "tag", "name")
    def __init__(self, eng, idx, fn, dma):
        self.eng = eng; self.idx = idx; self.fn = fn; self.dma = dma
        self.deps = []; self.sig = False; self.semi = None; self.val = None; self.tag = None; self.name = None

ENGS = ("pe", "act", "dve", "pool", "sp")
NDMASEM = 8

class Sched:
    def __init__(self):
        self.prog = {e: [] for e in ENGS}
        self.dmas = {e: [] for e in ENGS}
        self.tag = ""
    def add(self, eng, fn, reads=(), writes=(), dma=False):
        lst = self.prog[eng]
        ins = Ins(eng, len(lst), fn, dma)
        ins.tag = self.tag
        deps = {}
        def dep(p):
            if p is None or p is ins: return
            if not p.dma and p.eng == eng:
                if eng == "pe": return
            deps[id(p)] = p
        for r in reads: dep(r.w)
        for w in writes:
            dep(w.w)
            for q in w.rs: dep(q)
        for r in reads: r.rs.append(ins)
        for w in writes:
            w.w = ins; w.rs = []
        if dma:
            dl = self.dmas[eng]; k = len(dl)
            ins.semi = k % NDMASEM; ins.val = 16 * (k // NDMASEM + 1)
            ins.sig = True
            if k >= NDMASEM:
                prev = dl[k - NDMASEM]
                deps[id(prev)] = prev
            dl.append(ins)
        ins.deps = list(deps.values())
        for p in ins.deps: p.sig = True
        lst.append(ins)
        return ins
    def pe(self, fn, reads=(), writes=()): return self.add("pe", fn, reads, writes)
    def act(self, fn, reads=(), writes=()): return self.add("act", fn, reads, writes)
    def dve(self, fn, reads=(), writes=()): return self.add("dve", fn, reads, writes)
    def pool(self, fn, reads=(), writes=()): return self.add("pool", fn, reads, writes)
    def dma(self, eng, fn, reads=(), writes=()): return self.add(eng, fn, reads, writes, dma=True)

    def emit(self, nc, final_waits=()):
        with ExitStack() as es:
            csem = {e: es.enter_context(nc.semaphore("c_" + e)) for e in ENGS}
            dsem = {e: [es.enter_context(nc.semaphore("d_%s%d" % (e, i))) for i in range(NDMASEM)]
                    for e in ENGS if len(self.dmas[e]) > 0}
            for e in ENGS:
                c = 0
                for ins in self.prog[e]:
                    if ins.dma: continue
                    if ins.sig:
                        c += 1; ins.val = c
            block = es.enter_context(nc.Block())
            def semof(p):
                return dsem[p.eng][p.semi] if p.dma else csem[p.eng]
            def body(ename, eng):
                waited = {}
                for ins in self.prog[ename]:
                    for p in ins.deps:
                        s = semof(p); key = (p.eng, p.semi if p.dma else -1)
                        if waited.get(key, 0) >= p.val: continue
                        eng.wait_ge(s, p.val); waited[key] = p.val
                    bi = ins.fn(eng)
                    ins.name = bi.ins.name
                    if ins.sig:
                        if ins.dma: bi.then_inc(dsem[ename][ins.semi], 16)
                        else: bi.then_inc(csem[ename], 1)
                if ename == "sp":
                    for p in final_waits:
                        eng.wait_ge(semof(p), p.val)
            @block.tensor
            def _(e): body("pe", e)
            @block.scalar
            def _(e): body("act", e)
            @block.vector
            def _(e): body("dve", e)
            @block.gpsimd
            def _(e): body("pool", e)
            @block.sync
            def _(e): body("sp", e)

def MM(out, lhsT, rhs, start=True, stop=True):
    return lambda e: e.matmul(out, lhsT=lhsT, rhs=rhs, start=start, stop=stop)
def TR(out, in_, ident):
    return lambda e: e.transpose(out, in_, ident)
def ACT(out, in_, func, bias=None, scale=None, accum_out=None):
    kw = {}
    if bias is not None: kw["bias"] = bias
    if scale is not None: kw["scale"] = scale
    if accum_out is not None: kw["accum_out"] = accum_out
    return lambda e: e.activation(out=out, in_=in_, func=func, **kw)
def TT(out, in0, in1, op):
    return lambda e: e.tensor_tensor(out=out, in0=in0, in1=in1, op=op)
def TS(out, in0, s1, s2, op0, op1=None):
    if op1 is None:
        return lambda e: e.tensor_scalar(out=out, in0=in0, scalar1=s1, scalar2=None, op0=op0)
    return lambda e: e.tensor_scalar(out=out, in0=in0, scalar1=s1, scalar2=s2, op0=op0, op1=op1)
def STT(out, in0, scalar, in1, op0, op1):
    return lambda e: e.scalar_tensor_tensor(out=out, in0=in0, scalar=scalar, in1=in1, op0=op0, op1=op1)
def CP(out, in_):
    return lambda e: e.tensor_copy(out=out, in_=in_)
def RCP(out, in_):
    return lambda e: e.reciprocal(out=out, in_=in_)
def RSUM(out, in_):
    return lambda e: e.reduce_sum(out=out, in_=in_, axis=AX.X)
def MSET(ap, v):
    return lambda e: e.memset(ap, v)
def DMA(out, in_):
    return lambda e: e.dma_start(out=out, in_=in_)

D = 1024; HID = 2816; NHC = 22; TT_ = 512; P = 128
EPS = 1e-6
WIN_C = 2368; WUQ_C = 960; WUKV_C = 1152; WGU_C = 5632; ADA_C = 6144
NV = 81
NBR = 518
NCON = 4 + 128 + 128 + 128
SLOT_ELEMS = 4096
NSLOT = 3

def build(L, NS, S, DEPTH_TOTAL=None):
    NT = S // TT_
    NB = S // 128
    nc = bass.Bass("TRN2", target_bir_lowering=False)
    din = lambda n, s, d: nc.dram_tensor(n, s, d, kind="ExternalInput").ap()
    xT = din("xT", [NS, D, S], F32)
    cT = din("cT", [P, 8, NS], F32)
    pos = din("pos", [NS, S], I32)
    win = din("win", [L, D, WIN_C], F32)
    wuq = din("wuq", [L, 384, WUQ_C], F32)
    wukv = din("wukv", [L, 256, WUKV_C], F32)
    wout = din("wout", [L, D, D], F32)
    wgu = din("wgu", [L, D, WGU_C], F32)
    wdn = din("wdn", [L, HID, D], F32)
    ada = din("ada", [L, D, ADA_C], F32)
    wsT = din("wsT", [L, P, 4, P], F32)
    pv = din("pv", [L, P, NV], F32)
    bro = din("bro", [L, NBR], F32)
    fnw = din("fnw", [P, 8], F32)
    con = din("con", [P, NCON], F32)
    outT = nc.dram_tensor("outT", [NS, D, S], F32, kind="ExternalOutput").ap()
    dsc = lambda n, s, d: nc.dram_tensor(n, s, d, kind="Internal").ap()
    win_b = dsc("win_b", [L, D, WIN_C], BF16)
    wuq_b = dsc("wuq_b", [L, 384, WUQ_C], BF16)
    wukv_b = dsc("wukv_b", [L, 256, WUKV_C], BF16)
    wout_b = dsc("wout_b", [L, D, D], BF16)
    wgu_b = dsc("wgu_b", [L, D, WGU_C], BF16)
    wdn_b = dsc("wdn_b", [L, HID, D], BF16)
    ada_b = dsc("ada_b", [L, D, ADA_C], BF16)
    xs = dsc("xs", [NS, D, S], F32)
    tabd = dsc("tabd", [NS, S // TT_, P, 4 * TT_], BF16)

    S_ = Sched()
    es = ExitStack()
    with es:
        sbt = lambda n, s, d: es.enter_context(nc.sbuf_tensor(n, s, d))
        ring = [sbt("ring%d" % i, [P, SLOT_ELEMS], BF16) for i in range(NSLOT)]
        ringB = [Buf("ring%d" % i) for i in range(NSLOT)]
        KT = sbt("KT", [P, 6, S], BF16); KTB = [Buf("KT%d" % t) for t in range(NT)]
        VA = sbt("VA", [P, NB, 3, 192], BF16); VAB = [Buf("VA%d" % t) for t in range(NT)]
        xt = sbt("xt", [P, 8, TT_], F32); xtB = [Buf("xt%d" % c) for c in range(8)]
        hb = sbt("hb", [P, 8, TT_], BF16); hbB = [Buf("hb%d" % c) for c in range(8)]
        yT = sbt("yT", [P, 8, TT_], BF16); yTB = [Buf("yT%d" % c) for c in range(8)]
        NHH = 11
        hid = sbt("hid", [P, NHH, TT_], BF16); hidB = [Buf("hid%d" % c) for c in range(NHH)]
        sqb = yT; sqbB = yTB
        QA = hid[:, 0:6, :]; QAB = hidB[0:6]
        cqb = hid[:, 6:9, :]; cqbB = hidB[6:9]
        ckvb = hid[:, 9:11, :]; ckvbB = hidB[9:11]
        sqq = yT[:, 0:3, :]; sqqB = yTB[0:3]
        sqkv = yT[:, 3:5, :]; sqkvB = yTB[3:5]
        Qa = yT[:, 5:8, :]; QaB = yTB[5:8]
        Ka = sbt("Ka", [P, 5 * 128], BF16); KaB = Buf("Ka")
        Va = sbt("Va", [P, 5, 192], BF16); VaB = Buf("Va")
        yraw = sbt("yraw", [P, 3, TT_], F32); yrawB = Buf("yraw")
        NPT = 5
        pts = [sbt("pt%d" % i, [P, TT_], BF16) for i in range(NPT)]; ptB = [Buf("pt%d" % i) for i in range(NPT)]
        NTMP = 4
        tmps = [sbt("tmp%d" % i, [P, TT_], F32) for i in range(NTMP)]; tmpB = [Buf("tmp%d" % i) for i in range(NTMP)]
        rstd1 = sbt("rstd1", [P, TT_], F32); rstd1B = Buf("rstd1")
        rstdq = sbt("rstdq", [P, TT_], F32); rstdqB = Buf("rstdq")
        rstdkv = sbt("rstdkv", [P, TT_], F32); rstdkvB = Buf("rstdkv")
        rkvc = sbt("rkvc", [P, 4], F32); rkvcB = Buf("rkvc")
        tabs = sbt("tabs", [P, 4, TT_], BF16); tabB = [Buf("tab%d" % i) for i in range(4)]
        posi = sbt("posi", [P, TT_], I32); posf = yraw[:, 0, :]; posB = Buf("pos")
        ki = posi; kiB = Buf("ki")
        guv = sbt("guv", [P, 4, 512], BF16); guvB = [Buf("guv%d" % j) for j in range(4)]
        vnb = sbt("vnb", [P, 4, 256], BF16); vnbB = [Buf("vnb%d" % j) for j in range(4)]
        ycn = sbt("ycn", [P, 4, 256], BF16); ycnB = [Buf("ycn%d" % j) for j in range(4)]
        dsa = sbt("dsa", [P, 3, TT_], F32); dsaB = Buf("dsa"); dsb = dsa[:, 0, :]; dsbB = dsaB
        st = sbt("st", [P, 32], F32); stB = Buf("st")
        pvt = sbt("pvt", [P, NV], F32); pvtB = Buf("pvt")
        brt = sbt("brt", [P, NBR], F32); brtB = Buf("brt")
        esk = sbt("esk", [P, 6], F32); eskB = Buf("esk")
        wst = sbt("wst", [P, 4, P], BF16); wstB = Buf("wst")
        modt = sbt("modt", [P, NS, 48], F32); modB = Buf("mod")
        modn = sbt("modn", [P, NS, 48], F32); modnB = Buf("modn")
        gn = sbt("gn", [P, NS, 16], F32); gnB = Buf("gn")
        cact = sbt("cact", [P, 8, NS], BF16); cf = sbt("cf", [P, 8, NS], F32); cactB = Buf("cact")
        cons = sbt("cons", [P, NCON], F32); consB = Buf("cons")
        fnwt = sbt("fnwt", [P, 8], F32)
        onesb = sbt("onesb", [P, P], BF16); identb = sbt("identb", [P, P], BF16); onesf = sbt("onesf", [P, P], F32)
        mD = sbt("mD", [P, 3, P], BF16); mP = sbt("mP", [P, 3, P], BF16)
        psb = [es.enter_context(nc.psum_tensor("psb%d" % i, [P, TT_], F32)) for i in range(7)]
        psB = [Buf("psb%d" % i) for i in range(7)]
        pst = es.enter_context(nc.psum_tensor("pst", [P, 1024], BF16)); pstB = Buf("pst")
        NGEN = 4
        state = {"g": 0, "tmp": 0, "pt": 0, "acc": 0, "pacc": 0}
        def bank():
            i = state["g"] % NGEN; state["g"] += 1
            return psb[i], psB[i]
        def accbank():
            i = NGEN + state["acc"] % 2; state["acc"] += 1
            return psb[i], psB[i]
        def tmp():
            i = state["tmp"] % NTMP; state["tmp"] += 1
            return tmps[i], tmpB[i]
        def ptile():
            i = state["pt"] % NPT; state["pt"] += 1
            return pts[i], ptB[i]
        denb, denbB = psb[6], psB[6]

        tabdB = [[Buf("tabd%d_%d" % (s, t)) for t in range(NT)] for s in range(NS)]
        xsB = [[[Buf("xs%d_%d_%d" % (s, t, c)) for c in range(8)] for t in range(NT)] for s in range(NS)]

        S_.dma("sp", DMA(cons[:], con[:, :]), [], [consB])
        S_.dma("sp", DMA(cf[:], cT[:, :, :]), [], [cactB])
        S_.dma("sp", DMA(fnwt[:], fnw[:, :]), [], [consB])
        S_.act(ACT(cact[:], cf[:], AF.Silu), [cactB], [cactB])
        S_.dve(MSET(onesb[:], 1.0), [], [consB])
        S_.dve(MSET(onesf[:], 1.0), [], [consB])
        S_.dve(CP(identb[:], cons[:, 260:388]), [consB], [consB])
        for r in range(3):
            S_.dve(TS(mD[:, r, :], cons[:, 4:132], 1.0, 30000.0, ALU.subtract, ALU.mult), [consB], [consB])
            S_.dve(TS(mP[:, r, :], cons[:, 132:260], 1.0, 30000.0, ALU.subtract, ALU.mult), [consB], [consB])
        S_.pool(MSET(KT[32:64, :, :], 0.0), [], KTB)
        S_.pool(MSET(VA[:, :, :, 64:128], 1.0), [], VAB)
        S_.pool(MSET(Va[:], 1.0), [], [VaB])
        invA = cons[:, 0:1]; invB = cons[:, 1:2]; sgnA = cons[:, 2:3]; sgnB = cons[:, 3:4]

        pieces = {}
        def cast_rows(name, src, dst, l, rows):
            lst = []
            r0 = 0
            while r0 < rows:
                r1 = min(rows, r0 + 128)
                pb_ = Buf()
                S_.dma("pool", DMA(dst[l, r0:r1, :], src[l, r0:r1, :]), [], [pb_])
                lst.append(pb_)
                r0 = r1
            pieces[(name, l)] = lst
        for l in range(L):
            cast_rows("ada", ada, ada_b, l, D)
            cast_rows("win", win, win_b, l, D)
            cast_rows("wuq", wuq, wuq_b, l, 384)
            cast_rows("wukv", wukv, wukv_b, l, 256)
            cast_rows("wout", wout, wout_b, l, D)
            cast_rows("wgu", wgu, wgu_b, l, D)
            cast_rows("wdn", wdn, wdn_b, l, HID)

        slabs = []
        def add_slab(name, l, src2d, k0, kcn, c0, c1):
            assert kcn * (c1 - c0) <= SLOT_ELEMS
            slabs.append((name, l, src2d, k0, kcn, c0, c1))
            return len(slabs) - 1
        loaded = {"n": 0}
        def issue_load(i):
            name, l, src2d, k0, kcn, c0, c1 = slabs[i]
            slot = i % NSLOT
            ncol = c1 - c0
            dst = ring[slot][:, 0:kcn * ncol].rearrange("p (k n) -> p k n", n=ncol)
            srcv = src2d[k0 * 128:(k0 + kcn) * 128, c0:c1].rearrange("(k p) n -> p k n", p=128)
            S_.dma("sp", DMA(dst, srcv), pieces[(name, l)], [ringB[slot]])
        def use_slab(i):
            while loaded["n"] < min(len(slabs), i + NSLOT):
                issue_load(loaded["n"]); loaded["n"] += 1
            name, l, src2d, k0, kcn, c0, c1 = slabs[i]
            slot = i % NSLOT
            ncol = c1 - c0
            return ring[slot][:, 0:kcn * ncol].rearrange("p (k n) -> p k n", n=ncol), ringB[slot]

        GU_GROUPS = [(0, 2), (2, 2), (4, 2), (6, 2), (8, 2), (10, 1), (11, 2), (13, 2), (15, 2), (17, 2), (19, 2), (21, 1)]
        NADA = 12
        plan = {}
        ada_slab = lambda l, i: add_slab("ada", l, ada_b[l], 0, 8, i * 512, (i + 1) * 512)
        for l in range(L):
            if l == 0:
                plan[("ada", 0)] = [ada_slab(0, i) for i in range(NADA)]
            for s in range(NS):
                for t in range(NT):
                    k = (l, s, t)
                    hoist = (l + 1 < L) and s == NS - 1 and t == NT - 1
                    if hoist: plan[("ada", l + 1)] = []
                    plan[("in0",) + k] = add_slab("win", l, win_b[l], 0, 8, 0, 512)
                    plan[("in0b",) + k] = add_slab("win", l, win_b[l], 0, 8, 512, 1024)
                    plan[("in1",) + k] = add_slab("win", l, win_b[l], 0, 8, 1024, 1536)
                    plan[("in1b",) + k] = add_slab("win", l, win_b[l], 0, 8, 1536, 1728)
                    plan[("in2",) + k] = add_slab("win", l, win_b[l], 0, 8, 1728, 2240)
                    plan[("in2b",) + k] = add_slab("win", l, win_b[l], 0, 8, 2240, 2368)
                    plan[("uq",) + k] = add_slab("wuq", l, wuq_b[l], 0, 3, 0, WUQ_C)
                    plan[("ukv",) + k] = add_slab("wukv", l, wukv_b[l], 0, 2, 0, WUKV_C)
                    plan[("out",) + k] = [add_slab("wout", l, wout_b[l], 0, 8, i * 512, (i + 1) * 512) for i in range(2)]
                    for hf in range(2):
                        lst = []
                        for gi, (h0, n) in enumerate(GU_GROUPS[hf * 6:(hf + 1) * 6]):
                            lst.append(add_slab("wgu", l, wgu_b[l], 0, 8, h0 * 256, (h0 + n) * 256))
                            if hoist: plan[("ada", l + 1)].append(ada_slab(l + 1, hf * 6 + gi))
                        plan[("gu", hf) + k] = lst
                        plan[("dn", hf) + k] = [add_slab("wdn", l, wdn_b[l], hf * 11, 11, i * 256, (i + 1) * 256) for i in range(4)]

        def rstd_from_ssq(ps_ap, psbuf, out_ap, outbuf, n, small=False):
            if small:
                S_.act(ACT(out_ap, ps_ap, AF.Sqrt, bias=EPS, scale=1.0 / n), [psbuf], [outbuf])
                S_.dve(RCP(out_ap, out_ap), [outbuf], [outbuf])
            else:
                S_.act(ACT(out_ap, ps_ap, AF.Ln, bias=EPS, scale=1.0 / n), [psbuf], [outbuf])
                S_.act(ACT(out_ap, out_ap, AF.Exp, scale=-0.5), [outbuf], [outbuf])

        def norm_mod(gcol, shcol, s):
            for c in range(8):
                S_.act(ACT(sqb[:, c, :], xt[:, c, :], AF.Square), [xtB[c]], [sqbB[c]])
            pb, pB = bank()
            for c in range(8):
                S_.pe(MM(pb[:], onesb[:], sqb[:, c, :], c == 0, c == 7), [sqbB[c], consB], [pB])
            rstd_from_ssq(pb[:], pB, rstd1[:], rstd1B, float(D))
            for c in range(8):
                tm, tB = tmp()
                S_.dve(TT(tm[:], xt[:, c, :], rstd1[:], ALU.mult), [xtB[c], rstd1B], [tB])
                S_.act(ACT(hb[:, c, :], tm[:], AF.Identity, bias=modt[:, s, shcol + c:shcol + c + 1],
                           scale=gn[:, s, gcol + c:gcol + c + 1]), [tB, modB, gnB], [hbB[c]])

        def proj_fm(slab, sB, col0, ncols, rhs_tile, rhsB, nk):
            pb, pB = bank()
            for k in range(nk):
                S_.pe(MM(pb[0:ncols, :], slab[:, k, col0:col0 + ncols], rhs_tile[:, k, :], k == 0, k == nk - 1),
                      [sB, rhsB[k]], [pB])
            return pb, pB

        def rope_tables(s, tok0):
            S_.dma("pool", DMA(posi[:], pos[s, tok0:tok0 + TT_].partition_broadcast(P)), [], [posB, kiB])
            S_.dve(CP(posf, posi[:]), [posB], [yrawB])
            for ti, (inv, sg, addq) in enumerate(((invA, None, 0.25), (invA, sgnA, 0.0), (invB, None, 0.25), (invB, sgnB, 0.0))):
                tm, tB = tmp()
                S_.dve(TS(tm[:], posf, inv, None, ALU.mult), [yrawB, consB], [tB])
                if addq != 0.0:
                    S_.dve(TS(tm[:], tm[:], addq, None, ALU.add), [tB], [tB])
                S_.dve(CP(ki[:], tm[:]), [tB, posB], [kiB])
                tm2, tB2 = tmp()
                S_.dve(CP(tm2[:], ki[:]), [kiB], [tB2])
                S_.dve(TT(tm[:], tm[:], tm2[:], ALU.subtract), [tB, tB2], [tB])
                S_.act(ACT(tabs[:, ti, :], tm[:], AF.Sin, scale=(sg if sg is not None else 2.0 * math.pi)),
                       [tB, consB], [tabB[ti]])

        def rope_combine(pa, paB, pr, prB, rows, cos_ap, sin_ap, cB, out_ap, outB, post=None, postB=None):
            t1, t1B = tmp()
            S_.dve(TT(t1[rows, :], pa[rows, :], cos_ap, ALU.mult), [paB] + cB, [t1B])
            t2, t2B = tmp()
            S_.dve(TT(t2[rows, :], pr[rows, :], sin_ap, ALU.mult), [prB] + cB, [t2B])
            if post is None:
                S_.pool(TT(out_ap, t1[rows, :], t2[rows, :], ALU.add), [t1B, t2B], outB)
            else:
                S_.pool(TT(t1[rows, :], t1[rows, :], t2[rows, :], ALU.add), [t1B, t2B], [t1B])
                S_.pool(TT(out_ap, t1[rows, :], post, ALU.mult), [t1B, postB], outB)

        def group_norm_sq():
            S_.pool(TT(hb[:, 0:3, :], yraw[:], yraw[:], ALU.mult), [yrawB], hbB[0:3])
        def group_norm_fin(n_feat, ychunk0):
            pb, pB = bank()
            for c in range(3):
                S_.pe(MM(pb[:], onesb[:], hb[:, c, :], c == 0, c == 2), [hbB[c], consB], [pB])
            tm, tB = tmp()
            rstd_from_ssq(pb[:], pB, tm[:], tB, float(n_feat))
            for j in range(3):
                S_.dve(STT(yT[:, ychunk0 + j, :], yraw[:, j, :], pvt[:, 21 + ychunk0 + j:22 + ychunk0 + j], tm[:],
                           ALU.mult, ALU.mult), [yrawB, pvtB, tB], [yTB[ychunk0 + j]])

        def emit_ada(l_, i):
            slab, sB = use_slab(plan[("ada", l_)][i])
            pb, pB = bank()
            for c in range(4):
                for k in range(8):
                    S_.pe(MM(pb[:, c * NS:(c + 1) * NS], slab[:, k, c * 128:(c + 1) * 128], cact[:, k, :], k == 0, k == 7),
                          [sB, cactB], [pB])
            for s_ in range(NS):
                S_.dve(CP(modn[:, s_, i * 4:(i + 1) * 4], pb[:, 0:4 * NS].rearrange("p (m s) -> p s m", s=NS)[:, s_, :]), [pB], [modnB])

        final_dmas = []
        for l in range(L):
            last_layer = (l == L - 1)
            S_.tag = "prologue"
            S_.dma("pool", DMA(pvt[:], pv[l, :, :]), [], [pvtB])
            S_.dma("pool", DMA(brt[:], bro[l, :].partition_broadcast(P)), [], [brtB])
            tmw, tmwB = tmp()
            S_.dma("pool", DMA(tmw[:].rearrange("p (g t) -> p g t", t=P), wsT[l, :, :, :]), [], [tmwB])
            for g in range(4):
                S_.dve(TT(wst[:, g, :], tmw[:, g * P:(g + 1) * P], cons[:, 4:132], ALU.mult), [tmwB, consB], [wstB])
            S_.act(ACT(esk[:], brt[:, 0:6], AF.Exp), [brtB], [eskB])
            if l == 0:
                for i in range(NADA):
                    emit_ada(0, i)
            for s in range(NS):
                S_.dve(TT(modt[:, s, :], modn[:, s, :], pvt[:, 29:77], ALU.add), [modnB, pvtB], [modB])
                S_.dve(STT(gn[:, s, 0:8], modt[:, s, 8:16], 1.0, pvt[:, 0:8], ALU.add, ALU.mult), [modB, pvtB], [gnB])
                S_.dve(STT(gn[:, s, 8:16], modt[:, s, 32:40], 1.0, pvt[:, 8:16], ALU.add, ALU.mult), [modB, pvtB], [gnB])

            for s in range(NS):
                for t in range(NT):
                    key = (l, s, t)
                    tok0 = t * TT_
                    src = xT if l == 0 else xs
                    if l > 0:
                        S_.dma("pool", DMA(tabs[:].rearrange("p a b -> p (a b)"), tabd[s, t, :, :]), [tabdB[s][t]], tabB)
                    for c in range(8):
                        S_.dma("pool", DMA(xt[:, c, :], src[s, c * P:(c + 1) * P, tok0:tok0 + TT_]), [xsB[s][t][c]], [xtB[c]])
                    cosA = tabs[:, 0, :]; sinA = tabs[:, 1, :]
                    S_.tag = "norm1"
                    norm_mod(0, 0, s)
                    S_.tag = "load_rope"
                    if l == 0:
                        rope_tables(s, tok0)
                        if L > 1:
                            S_.dma("pool", DMA(tabd[s, t, :, :], tabs[:].rearrange("p a b -> p (a b)")), tabB, [tabdB[s][t]])
                    S_.tag = "win_a"
                    sl0, sB0 = use_slab(plan[("in0",) + key])
                    if t > 0:
                        S_.pool(CP(Ka[:, 0:128], Ka[:, 512:640]), [KaB], [KaB])
                        S_.pool(CP(Va[:, 0, :], Va[:, 4, :]), [VaB], [VaB])
                    for j in range(2):
                        pa, paB = proj_fm(sl0, sB0, (2 * j) * 128, 128, hb, hbB, 8)
                        pr, prB = proj_fm(sl0, sB0, (2 * j + 1) * 128, 128, hb, hbB, 8)
                        rope_combine(pa, paB, pr, prB, slice(0, P), cosA, sinA, [tabB[0], tabB[1]], Qa[:, j, :], [QaB[j]])
                    sl0b, sB0b = use_slab(plan[("in0b",) + key])
                    pa, paB = proj_fm(sl0b, sB0b, 0, 128, hb, hbB, 8)
                    pr, prB = proj_fm(sl0b, sB0b, 128, 128, hb, hbB, 8)
                    rope_combine(pa, paB, pr, prB, slice(0, P), cosA, sinA, [tabB[0], tabB[1]], Qa[:, 2, :], [QaB[2]])
                    pa, paB = proj_fm(sl0b, sB0b, 256, 128, hb, hbB, 8)
                    pr, prB = proj_fm(sl0b, sB0b, 384, 128, hb, hbB, 8)
                    rope_combine(pa, paB, pr, prB, slice(0, P), cosA, sinA, [tabB[0], tabB[1]], Ka[:, 128:640], [KaB])
                    S_.tag = "win_b"
                    sl1, sB1 = use_slab(plan[("in1",) + key])
                    for j in range(3):
                        pa, paB = proj_fm(sl1, sB1, j * 128, 128, hb, hbB, 8)
                        S_.act(ACT(cqb[:, j, :], pa[:], AF.Copy, scale=pvt[:, 16 + j:17 + j]), [paB, pvtB], [cqbB[j]])
                        S_.act(ACT(sqq[:, j, :], pa[:], AF.Square), [paB], [sqqB[j]])
                    sl1b, sB1b = None, None
                    for j in range(2):
                        if j == 0:
                            pa, paB = proj_fm(sl1, sB1, 384, 128, hb, hbB, 8)
                        else:
                            sl1b, sB1b = use_slab(plan[("in1b",) + key])
                            pa, paB = proj_fm(sl1b, sB1b, 0, 128, hb, hbB, 8)
                        S_.act(ACT(ckvb[:, j, :], pa[:], AF.Copy, scale=pvt[:, 19 + j:20 + j]), [paB, pvtB], [ckvbB[j]])
                        S_.act(ACT(sqkv[:, j, :], pa[:], AF.Square), [paB], [sqkvB[j]])
                    pb, pB = bank()
                    for c in range(3):
                        S_.pe(MM(pb[:], onesb[:], sqq[:, c, :], c == 0, c == 2), [sqqB[c], consB], [pB])
                    rstd_from_ssq(pb[:], pB, rstdq[:], rstdqB, 384.0)
                    pb, pB = bank()
                    for c in range(2):
                        S_.pe(MM(pb[:], onesb[:], sqkv[:, c, :], c == 0, c == 1), [sqkvB[c], consB], [pB])
                    rstd_from_ssq(pb[:], pB, rstdkv[:], rstdkvB, 256.0)
                    pb, pB = bank()
                    for j in range(4):
                        for c in range(2):
                            S_.pe(MM(pb[:, j:j + 1], sqkv[:, c, j * 128:(j + 1) * 128], onesb[:, 0:1], c == 0, c == 1),
                                  [sqkvB[c], consB], [pB])
                    rstd_from_ssq(pb[:, 0:4], pB, rkvc[:], rkvcB, 256.0, small=True)
                    cosB = tabs[0:32, 2, :]; sinB = tabs[0:32, 3, :]
                    pa, paB = proj_fm(sl1b, sB1b, 128, 32, hb, hbB, 8)
                    pr, prB = proj_fm(sl1b, sB1b, 160, 32, hb, hbB, 8)
                    rope_combine(pa, paB, pr, prB, slice(0, 32), cosB, sinB, [tabB[2], tabB[3]],
                                 KT[0:32, 0, tok0:tok0 + TT_], [KTB[t]])
                    for h in range(1, 6):
                        S_.pool(CP(KT[0:32, h, tok0:tok0 + TT_], KT[0:32, 0, tok0:tok0 + TT_]), [KTB[t]], [KTB[t]])
                    S_.tag = "win_tok"
                    sl2, sB2 = use_slab(plan[("in2",) + key])
                    for j in range(4):
                        pb, pB = bank()
                        for k in range(8):
                            S_.pe(MM(pb[:], hb[:, k, j * 128:(j + 1) * 128], sl2[:, k, 0:512], k == 0, k == 7), [hbB[k], sB2], [pB])
                        S_.act(ACT(guv[:, j, :], pb[:], AF.Gelu), [pB], [guvB[j]])
                        if j == 0:
                            S_.dve(MSET(st[:], 0.0), [], [stB])
                        tmq, tqB = tmp()
                        S_.act(ACT(tmq[:, 0:256], guv[:, j, 256:512], AF.Square, accum_out=st[:, 4 + j:5 + j]), [guvB[j], stB], [tqB, stB])
                        S_.dve(RSUM(st[:, j:j + 1], guv[:, j, 256:512]), [guvB[j]], [stB])
                    sl2b, sB2b = use_slab(plan[("in2b",) + key])
                    for j in range(4):
                        pb2, pB2 = bank()
                        for k in range(8):
                            S_.pe(MM(pb2[:, 0:128], hb[:, k, j * 128:(j + 1) * 128], sl2b[:, k, 0:128], k == 0, k == 7), [hbB[k], sB2b], [pB2])
                        S_.dve(CP(Va[:, 1 + j, :].rearrange("p (a b) -> p a b", b=64)[:, 0::2, :],
                                  pb2[:, 0:128].rearrange("p (a b) -> p a b", b=64)), [pB2], [VaB])
                    S_.tag = "mla_up"
                    slq, sBq = use_slab(plan[("uq",) + key])
                    for h in range(6):
                        pa, paB = proj_fm(slq, sBq, h * 128, 128, cqb, cqbB, 3)
                        pr, prB = proj_fm(slq, sBq, 768 + h * 32, 32, cqb, cqbB, 3)
                        S_.dve(TT(QA[:, h, :], pa[:], rstdq[:], ALU.mult), [paB, rstdqB], [QAB[h]])
                        rope_combine(pa, paB, pr, prB, slice(0, 32), cosB, sinB, [tabB[2], tabB[3]],
                                     QA[0:32, h, :], [QAB[h]], post=rstdq[0:32, :], postB=rstdqB)
                    slk, sBk = use_slab(plan[("ukv",) + key])
                    for h in range(6):
                        pa, paB = proj_fm(slk, sBk, h * 128, 128, ckvb, ckvbB, 2)
                        S_.dve(TT(KT[64:128, h, tok0:tok0 + TT_], pa[64:128, :], rstdkv[64:128, :], ALU.mult),
                               [paB, rstdkvB], [KTB[t]])
                    for j in range(4):
                        pb, pB = bank()
                        for c in range(2):
                            S_.pe(MM(pb[:, 0:384], ckvb[:, c, j * 128:(j + 1) * 128], slk[:, c, 768:1152], c == 0, c == 1),
                                  [ckvbB[c], sBk], [pB])
                        bi = 4 * t + j
                        S_.act(ACT(VA[:, bi, :, 0:64], pb[:, 0:192].rearrange("p (a b) -> p a b", b=64), AF.Copy,
                                   scale=rkvc[:, j:j + 1]), [pB, rkvcB], [VAB[t]])
                        S_.act(ACT(VA[:, bi, :, 128:192], pb[:, 192:384].rearrange("p (a b) -> p a b", b=64), AF.Copy,
                                   scale=rkvc[:, j:j + 1]), [pB, rkvcB], [VAB[t]])

                    S_.tag = "swa"
                    iters = [(b, kv) for b in range(4) for kv in range(2)]
                    def swa_stage_a(b, kv):
                        gblk = 4 * t + b
                        rows = slice(kv * 64, (kv + 1) * 64)
                        kts = ([] if gblk == 0 else [(b, mP)]) + [(b + 1, mD)]
                        outl = []
                        for (blk, msk) in kts:
                            pb, pB = bank()
                            S_.pe(MM(pb[:, 0:384], Ka[rows, blk * 128:(blk + 1) * 128],
                                     Qa[rows, :, b * 128:(b + 1) * 128], True, False), [KaB] + QaB, [pB])
                            S_.pe(MM(pb[:, 0:384], identb[:], msk[:].rearrange("p a b -> p (a b)"), False, True), [consB], [pB])
                            pt, pB2 = ptile()
                            S_.act(ACT(pt[:, 0:384], pb[:, 0:384], AF.Exp, scale=0.125), [pB], [pB2])
                            outl.append((blk, pt, pB2))
                        return outl
                    def swa_stage_b(b, kv, pl):
                        rows = slice(kv * 64, (kv + 1) * 64)
                        drows = slice((1 - kv) * 64, (2 - kv) * 64)
                        acc, accB = accbank()
                        for ii, (blk, pt, pB2) in enumerate(pl):
                            vsl = Va[:, blk, 0:128] if kv == 0 else Va[:, blk, 64:192]
                            S_.pe(MM(acc[:, 0:384], vsl, pt[:, 0:384], ii == 0, ii == len(pl) - 1), [VaB, pB2], [accB])
                        for hh in range(3):
                            h = 3 * kv + hh
                            S_.dve(TS(dsa[rows, hh, b * 128:(b + 1) * 128], acc[drows, hh * 128:(hh + 1) * 128], esk[rows, h:h + 1], None, ALU.add),
                                   [accB, eskB], [dsaB])
                        S_.dve(CP(yraw[rows, :, b * 128:(b + 1) * 128], acc[rows, 0:384].rearrange("p (a b) -> p a b", b=128)), [accB], [yrawB])
                    pend = swa_stage_a(*iters[0])
                    for ii_ in range(len(iters)):
                        nxt = swa_stage_a(*iters[ii_ + 1]) if ii_ + 1 < len(iters) else None
                        swa_stage_b(iters[ii_][0], iters[ii_][1], pend)
                        pend = nxt
                    S_.act(ACT(dsa[:], dsa[:], AF.Ln), [dsaB], [dsaB])
                    S_.act(ACT(dsa[:], dsa[:], AF.Exp, scale=-1.0), [dsaB], [dsaB])
                    S_.dve(TT(yraw[:], yraw[:], dsa[:], ALU.mult), [yrawB, dsaB], [yrawB])
                    group_norm_sq()
                    S_.tag = "sgu1"
                    S_.dve(TS(st[:, 8:12], st[:, 0:4], 1.0 / 256, None, ALU.mult), [stB], [stB])
                    S_.dve(TT(st[:, 12:16], st[:, 8:12], st[:, 8:12], ALU.mult), [stB], [stB])
                    S_.dve(STT(st[:, 16:20], st[:, 4:8], 1.0 / 256, st[:, 12:16], ALU.mult, ALU.subtract), [stB], [stB])
                    S_.act(ACT(st[:, 16:20], st[:, 16:20], AF.Sqrt, bias=EPS), [stB], [stB])
                    S_.dve(RCP(st[:, 16:20], st[:, 16:20]), [stB], [stB])
                    for j in range(4):
                        tm, tB = tmp()
                        S_.dve(TS(tm[:, 0:256], guv[:, j, 256:512], st[:, 8 + j:9 + j], st[:, 16 + j:17 + j], ALU.subtract, ALU.mult),
                               [guvB[j], stB], [tB])
                        S_.pool(TT(tm[:, 0:256], tm[:, 0:256], brt[:, 6:262], ALU.mult), [tB, brtB], [tB])
                        S_.pool(TT(vnb[:, j, :], tm[:, 0:256], brt[:, 262:518], ALU.add), [tB, brtB], [vnbB[j]])
                    def hook_normA():
                        S_.tag = "normA"
                        group_norm_fin(384, 0)
                        S_.tag = "mla"
                    def hook_sgu2():
                        S_.tag = "sgu2"
                        for j in range(4):
                            pb, pB = bank()
                            for g in range(4):
                                S_.pe(MM(pb[:, g * 64:(g + 1) * 64], wst[:, g, :], vnb[:, j, g * 64:(g + 1) * 64]), [wstB, vnbB[j]], [pB])
                            ycj, ycjB = tmp()
                            for g in range(4):
                                S_.dve(STT(ycj[:, g * 64:(g + 1) * 64], pb[:, g * 64:(g + 1) * 64], pvt[:, 77 + g:78 + g],
                                           guv[:, j, g * 64:(g + 1) * 64], ALU.add, ALU.mult), [pB, pvtB, guvB[j]], [ycjB])
                            S_.act(ACT(ycj[:, 256:512], ycj[:, 0:256], AF.Square, accum_out=st[:, 20 + j:21 + j]), [ycjB, stB], [ycjB, stB])
                            S_.act(ACT(st[:, 24 + j:25 + j], st[:, 20 + j:21 + j], AF.Sqrt, bias=EPS, scale=1.0 / 256), [stB], [stB])
                            S_.dve(RCP(st[:, 24 + j:25 + j], st[:, 24 + j:25 + j]), [stB], [stB])
                            S_.act(ACT(ycn[:, j, :], ycj[:, 0:256], AF.Copy, scale=st[:, 24 + j:25 + j]), [ycjB, stB], [ycnB[j]])

                        S_.tag = "mla"
                    after_head = {0: hook_normA, 1: hook_sgu2}
                    S_.tag = "mla"
                    LA = 3
                    deferred = [None]
                    for h in range(6):
                        acc, accB = accbank()
                        nkt = 4 * (t + 1)
                        def mla_a(kt):
                            d = kt - 4 * t
                            q0 = 0 if d < 0 else d * 128
                            pb, pB = bank()
                            S_.pe(MM(pb[:, q0:TT_], KT[:, h, kt * 128:(kt + 1) * 128], QA[:, h, q0:TT_], True, d < 0),
                                  [KTB[kt // 4], QAB[h]], [pB])
                            if d >= 0:
                                S_.pe(MM(pb[:, q0:q0 + 128], identb[:], mD[:, 0, :], False, True), [consB], [pB])
                            pt, pB2 = ptile()
                            S_.act(ACT(pt[:, q0:TT_], pb[:, q0:TT_], AF.Exp, scale=96.0 ** -0.5), [pB], [pB2])
                            return (q0, pt, pB2)
                        def mla_b(kt, q0, pt, pB2):
                            vsl = VA[:, kt, h, 0:128] if h < 3 else VA[:, kt, h - 3, 64:192]
                            S_.pe(MM(acc[:, q0:TT_], vsl, pt[:, q0:TT_], kt == 0, kt == nkt - 1),
                                  [VAB[kt // 4], pB2], [accB])
                        pendq = []; nb_ = [0]
                        def do_b():
                            mla_b(*pendq.pop(0)); nb_[0] += 1
                            if nb_[0] == 1 and deferred[0] is not None:
                                deferred[0](); deferred[0] = None
                        for kt in range(nkt):
                            pendq.append((kt,) + mla_a(kt))
                            if len(pendq) > LA:
                                do_b()
                        while pendq:
                            do_b()
                        if h in after_head:
                            after_head[h]()
                        rows = slice(0, 64) if h < 3 else slice(64, 128)
                        drows = slice(64, 128) if h < 3 else slice(0, 64)
                        def mk_finish(rows, hh, acc, accB, drows=drows):
                            def fin():
                                S_.act(ACT(dsb[rows, :], acc[drows, :], AF.Ln), [accB], [dsbB])
                                S_.act(ACT(dsb[rows, :], dsb[rows, :], AF.Exp, scale=-1.0), [dsbB], [dsbB])
                                S_.dve(TT(yraw[rows, hh, :], acc[rows, :], dsb[rows, :], ALU.mult), [accB, dsbB], [yrawB])
                            return fin
                        deferred[0] = mk_finish(rows, h % 3, acc, accB)
                    deferred[0](); deferred[0] = None
                    group_norm_sq()
                    S_.tag = "sgu3"
                    for j in range(4):
                        for c in range(2):
                            S_.pe(TR(pst[:, (c * 4 + j) * 128:(c * 4 + j + 1) * 128], ycn[:, j, c * 128:(c + 1) * 128], identb[:]),
                                  [ycnB[j], consB], [pstB])
                    for c in range(2):
                        S_.act(ACT(yT[:, 6 + c, :], pst[:, c * 512:(c + 1) * 512], AF.Copy, scale=pvt[:, 27 + c:28 + c]),
                               [pstB, pvtB], [yTB[6 + c]])

                    S_.tag = "normB"
                    group_norm_fin(384, 3)
                    S_.tag = "outproj"
                    for i in range(2):
                        slo, sBo = use_slab(plan[("out",) + key][i])
                        for q in range(4):
                            fc = i * 4 + q
                            pb, pB = bank()
                            for kk, k in enumerate((0, 1, 2, 6, 7, 3, 4, 5)):
                                S_.pe(MM(pb[:], slo[:, k, q * 128:(q + 1) * 128], yT[:, k, :], kk == 0, kk == 7), [sBo, yTB[k]], [pB])
                            S_.dve(STT(xt[:, fc, :], pb[:], modt[:, s, 16 + fc:17 + fc], xt[:, fc, :], ALU.mult, ALU.add),
                                   [pB, modB, xtB[fc]], [xtB[fc]])

                    S_.tag = "ffn"
                    norm_mod(8, 24, s)
                    for hf in range(2):
                        for gi, (h0, n) in enumerate(GU_GROUPS[hf * 6:(hf + 1) * 6]):
                            slg, sBg = use_slab(plan[("gu", hf) + key][gi])
                            for q in range(n):
                                hl = h0 + q - hf * 11
                                pg, pgB = bank()
                                for k in range(8):
                                    S_.pe(MM(pg[:], slg[:, k, q * 256:q * 256 + 128], hb[:, k, :], k == 0, k == 7), [sBg, hbB[k]], [pgB])
                                pu, puB = bank()
                                for k in range(8):
                                    S_.pe(MM(pu[:], slg[:, k, q * 256 + 128:q * 256 + 256], hb[:, k, :], k == 0, k == 7), [sBg, hbB[k]], [puB])
                                tm, tB = tmp()
                                S_.act(ACT(tm[:], pg[:], AF.Silu), [pgB], [tB])
                                S_.dve(TT(hid[:, hl, :], pu[:], tm[:], ALU.mult), [puB, tB], [hidB[hl]])
                            if (l + 1 < L) and s == NS - 1 and t == NT - 1:
                                S_.tag = "ada"; emit_ada(l + 1, hf * 6 + gi); S_.tag = "ffn"
                        for i in range(4):
                            sld, sBd = use_slab(plan[("dn", hf) + key][i])
                            for q in range(2):
                                fc = i * 2 + q
                                pb, pB = bank()
                                for k in range(NHH):
                                    S_.pe(MM(pb[:], sld[:, k, q * 128:(q + 1) * 128], hid[:, k, :], k == 0, k == NHH - 1),
                                          [sBd, hidB[k]], [pB])
                                S_.dve(STT(xt[:, fc, :], pb[:], modt[:, s, 40 + fc:41 + fc], xt[:, fc, :], ALU.mult, ALU.add),
                                       [pB, modB, xtB[fc]], [xtB[fc]])
                    S_.tag = "store"
                    if last_layer:
                        for c in range(8):
                            S_.act(ACT(sqb[:, c, :], xt[:, c, :], AF.Square), [xtB[c]], [sqbB[c]])
                        pb, pB = bank()
                        for c in range(8):
                            S_.pe(MM(pb[:], onesb[:], sqb[:, c, :], c == 0, c == 7), [sqbB[c], consB], [pB])
                        rstd_from_ssq(pb[:], pB, rstd1[:], rstd1B, float(D))
                        for c in range(8):
                            S_.dve(STT(xt[:, c, :], xt[:, c, :], fnwt[:, c:c + 1], rstd1[:], ALU.mult, ALU.mult),
                                   [xtB[c], rstd1B, consB], [xtB[c]])
                            dm = S_.dma("pool", DMA(outT[s, c * P:(c + 1) * P, tok0:tok0 + TT_], xt[:, c, :]), [xtB[c]], [Buf()])
                            final_dmas.append(dm)
                    else:
                        for c in range(8):
                            S_.dma("pool", DMA(xs[s, c * P:(c + 1) * P, tok0:tok0 + TT_], xt[:, c, :]), [xtB[c]], [xsB[s][t][c]])
        S_.emit(nc, final_waits=final_dmas)
    return nc, S_

def _rot(cols, n):
    h = n // 2
    return np.concatenate([cols[..., h:], cols[..., :h]], axis=-1)

def prep_weights(inp, L):
    f = lambda a: np.ascontiguousarray(np.asarray(a, dtype=np.float32))
    w_in = f(inp["w_in"])
    win = np.zeros((L, D, WIN_C), np.float32)
    aq = lambda h: w_in[:, :, h * 64:(h + 1) * 64]
    ak = lambda k: w_in[:, :, 384 + k * 64:384 + (k + 1) * 64]
    for j in range(3):
        c0 = 2 * j * 128; c1 = (2 * j + 1) * 128
        win[:, :, c0:c0 + 64] = aq(j); win[:, :, c0 + 64:c0 + 128] = aq(j + 3)
        win[:, :, c1:c1 + 64] = _rot(aq(j), 64); win[:, :, c1 + 64:c1 + 128] = _rot(aq(j + 3), 64)
    win[:, :, 768:832] = ak(0); win[:, :, 832:896] = ak(1)
    win[:, :, 896:960] = _rot(ak(0), 64); win[:, :, 960:1024] = _rot(ak(1), 64)
    win[:, :, 1024:1408] = w_in[:, :, 640:1024]
    win[:, :, 1408:1664] = w_in[:, :, 1024:1280]
    win[:, :, 1664:1696] = w_in[:, :, 1280:1312]
    win[:, :, 1696:1728] = _rot(w_in[:, :, 1280:1312], 32)
    win[:, :, 1728:2240] = w_in[:, :, 1312:1824]
    win[:, :, 2240:2368] = w_in[:, :, 512:640]
    w_uq = f(inp["b_w_uq"])
    wuq = np.zeros((L, 384, WUQ_C), np.float32)
    for h in range(6):
        nope = w_uq[:, :, h * 96:h * 96 + 64]; rope = w_uq[:, :, h * 96 + 64:h * 96 + 96]
        wuq[:, :, h * 128:h * 128 + 32] = rope
        wuq[:, :, h * 128 + 64:h * 128 + 128] = nope
        wuq[:, :, 768 + h * 32:768 + (h + 1) * 32] = _rot(rope, 32)
    w_ukv = f(inp["b_w_ukv"])
    wukv = np.zeros((L, 256, WUKV_C), np.float32)
    for h in range(6):
        wukv[:, :, h * 128 + 64:h * 128 + 128] = w_ukv[:, :, h * 128:h * 128 + 64]
        wukv[:, :, 768 + h * 64:768 + (h + 1) * 64] = w_ukv[:, :, h * 128 + 64:h * 128 + 128]
    perm = []
    for base in (0, 384):
        for j in range(3):
            perm += list(range(base + j * 64, base + (j + 1) * 64)) + list(range(base + (j + 3) * 64, base + (j + 4) * 64))
    perm += list(range(768, 1024))
    perm = np.array(perm)
    wout = f(inp["w_out"])[:, perm, :]
    gw = f(inp["out_norm_w"])[:, perm]
    wgu_o = f(inp["w_gate_up"])
    wgu = np.zeros((L, D, WGU_C), np.float32)
    for hc in range(NHC):
        wgu[:, :, hc * 256:hc * 256 + 128] = wgu_o[:, :, hc * 128:(hc + 1) * 128]
        wgu[:, :, hc * 256 + 128:(hc + 1) * 256] = wgu_o[:, :, HID + hc * 128:HID + (hc + 1) * 128]
    wdn = f(inp["w_down"])
    ada = f(inp["ada_w"])
    wsT = np.ascontiguousarray(np.transpose(f(inp["c_w_s"]), (0, 3, 1, 2)))
    pm = lambda v, n: np.transpose(v.reshape(L, n, 128), (0, 2, 1))
    pv = np.zeros((L, P, NV), np.float32)
    pv[:, :, 0:8] = pm(f(inp["norm1_w"]), 8)
    pv[:, :, 8:16] = pm(f(inp["norm2_w"]), 8)
    pv[:, :, 16:19] = pm(f(inp["b_q_norm_w"]), 3)
    pv[:, :, 19:21] = pm(f(inp["b_kv_norm_w"]), 2)
    pv[:, :, 21:29] = pm(gw, 8)
    pv[:, :, 29:77] = pm(f(inp["ada_b"]), 48)
    pv[:, :, 77:81] = np.transpose(f(inp["c_b_s"]), (0, 2, 1))
    bro = np.concatenate([f(inp["a_sinks"]), f(inp["c_ln_w"]), f(inp["c_ln_b"])], axis=1)
    fnw = np.ascontiguousarray(f(inp["final_norm_w"]).reshape(8, 128).T)
    con = np.zeros((P, NCON), np.float32)
    p = np.arange(P)
    con[:, 0] = (10000.0 ** (-(2.0 * (p % 32)) / 64.0)) / (2 * np.pi)
    con[:, 1] = (10000.0 ** (-(2.0 * (p % 16)) / 32.0)) / (2 * np.pi)
    con[:, 2] = np.where((p % 64) < 32, -2 * np.pi, 2 * np.pi)
    con[:, 3] = np.where((p % 32) < 16, -2 * np.pi, 2 * np.pi)
    kk = p[:, None]; qq = p[None, :]
    con[:, 4:132] = (kk <= qq)
    con[:, 132:260] = (kk > qq)
    con[:, 260:388] = np.eye(P)
    return dict(win=win, wuq=wuq, wukv=wukv, wout=wout, wgu=wgu, wdn=wdn, ada=ada, wsT=wsT, pv=pv,
                bro=np.ascontiguousarray(bro), fnw=fnw, con=con)

def run(inp, L, NS, S, n_cores, trace=False):
    nc, _ = build(L, NS, S)
    wts = prep_weights(inp, L)
    x = np.asarray(inp["x"], np.float32); c = np.asarray(inp["c"], np.float32)
    posn = np.asarray(inp["positions"]).astype(np.int32)
    in_maps = []
    for i in range(n_cores):
        b0 = i * NS
        m = dict(wts)
        m["xT"] = np.ascontiguousarray(np.transpose(x[b0:b0 + NS], (0, 2, 1)))
        m["cT"] = np.ascontiguousarray(np.transpose(c[b0:b0 + NS].reshape(NS, 8, 128), (2, 1, 0)))
        m["pos"] = np.ascontiguousarray(posn[b0:b0 + NS])
        in_maps.append(m)
    res = run_bass_kernel_spmd(nc, in_maps, core_ids=list(range(n_cores)), **({"trace": True} if trace else {}))
    outs = [np.transpose(r["outT"], (0, 2, 1)) for r in res.results]
    return np.ascontiguousarray(np.concatenate(outs, axis=0)).astype(np.float32), res

def kernel(**inputs):
    out, _ = run(inputs, 4, 2, 4096, 8)
    return out
```
